# Optimizing a Trainium2 kernel written in Bass

```python
import jax
import jax.numpy as jnp
from jax import lax
import numpy as np


D_MODEL = 1024
BATCH = 4
SEQ = 8192
DEPTH = 1

N_HEADS = 16
HEAD_DIM = D_MODEL // N_HEADS
ATTN_WIDTH = N_HEADS * HEAD_DIM
CONV_CHANNELS = D_MODEL
CONV_WIDTH = 31
MOBA_BLOCK = 256
MOBA_TOPK = 3
Q_CHUNK = 32
D_FF = 2816
EPS = 1e-6
NEG_INF = -1e30
IN_COLS = 2 * CONV_CHANNELS + 3 * ATTN_WIDTH + 2 * D_MODEL
SPLITS = (
    CONV_CHANNELS,
    2 * CONV_CHANNELS,
    2 * CONV_CHANNELS + ATTN_WIDTH,
    2 * CONV_CHANNELS + 2 * ATTN_WIDTH,
    2 * CONV_CHANNELS + 3 * ATTN_WIDTH,
    2 * CONV_CHANNELS + 3 * ATTN_WIDTH + D_MODEL,
)

kernel_name = 'hybrid_conformer_conv_moba_block'


def rms_norm(x, g):
    xf = x.astype(jnp.float32)
    y = xf * lax.rsqrt(jnp.mean(xf * xf, axis=-1, keepdims=True) + EPS)
    return (y * g.astype(jnp.float32)).astype(x.dtype)


def layer_norm(x, g, b):
    xf = x.astype(jnp.float32)
    mu = jnp.mean(xf, axis=-1, keepdims=True)
    var = jnp.mean(jnp.square(xf - mu), axis=-1, keepdims=True)
    y = (xf - mu) * lax.rsqrt(var + EPS)
    return (y * g.astype(jnp.float32) + b.astype(jnp.float32)).astype(x.dtype)


def swiglu(h, w_gate, w_up, w_down):
    return (jax.nn.silu(h @ w_gate) * (h @ w_up)) @ w_down


def causal_depthwise_conv(u, w, b):
    y = lax.conv_general_dilated(
        u, w[:, None, :], window_strides=(1,), padding=[(CONV_WIDTH - 1, 0)],
        dimension_numbers=('NWC', 'WIO', 'NWC'), feature_group_count=u.shape[-1])
    return y + b


def moba_attention(q, k, v):
    b, s, h, d = q.shape
    s_pad = -(-s // MOBA_BLOCK) * MOBA_BLOCK
    nb = s_pad // MOBA_BLOCK
    topk = min(MOBA_TOPK, nb)
    pad = ((0, 0), (0, s_pad - s), (0, 0), (0, 0))
    q, k, v = [jnp.pad(t, pad).transpose(0, 2, 1, 3) for t in (q, k, v)]
    kb = k.reshape(b, h, nb, MOBA_BLOCK, d)
    vb = v.reshape(b, h, nb, MOBA_BLOCK, d)
    k_mean = jnp.mean(kb.astype(jnp.float32), axis=3)
    gate = jnp.einsum('bhsd,bhnd->bhsn', q.astype(jnp.float32), k_mean)
    q_blk = jnp.arange(s_pad) // MOBA_BLOCK
    fully_past = jnp.arange(nb)[None, :] < q_blk[:, None]
    gate = jnp.where(fully_past, gate, -jnp.inf)
    _, sel = lax.top_k(gate, topk)
    sel_valid = sel < q_blk[:, None]
    n_chunks = s_pad // Q_CHUNK
    scale = d ** -0.5
    bi = jnp.arange(b)[:, None, None, None]
    hi = jnp.arange(h)[None, :, None, None]

    def to_chunks(t):
        return jnp.moveaxis(t.reshape(b, h, n_chunks, Q_CHUNK, *t.shape[3:]), 2, 0)

    def step(args):
        c, q_c, sel_c, valid_c = args
        q0 = c * Q_CHUNK
        own = q0 // MOBA_BLOCK
        k_own = lax.dynamic_index_in_dim(kb, own, axis=2, keepdims=False)
        v_own = lax.dynamic_index_in_dim(vb, own, axis=2, keepdims=False)
        q_pos = q0 + jnp.arange(Q_CHUNK)
        k_pos = own * MOBA_BLOCK + jnp.arange(MOBA_BLOCK)
        s_own = jnp.einsum('bhqd,bhkd->bhqk', q_c, k_own).astype(jnp.float32) * scale
        s_own = jnp.where(k_pos[None, :] <= q_pos[:, None], s_own, NEG_INF)
        k_sel = kb[bi, hi, sel_c]
        v_sel = vb[bi, hi, sel_c]
        s_sel = jnp.einsum('bhqd,bhqnkd->bhqnk', q_c, k_sel).astype(jnp.float32) * scale
        s_sel = jnp.where(valid_c[..., None], s_sel, NEG_INF)
        logits = jnp.concatenate([s_own, s_sel.reshape(b, h, Q_CHUNK, topk * MOBA_BLOCK)], axis=-1)
        p = jax.nn.softmax(logits, axis=-1)
        p_own = p[..., :MOBA_BLOCK]
        p_sel = p[..., MOBA_BLOCK:].reshape(b, h, Q_CHUNK, topk, MOBA_BLOCK)
        o = (jnp.einsum('bhqk,bhkd->bhqd', p_own, v_own.astype(jnp.float32))
             + jnp.einsum('bhqnk,bhqnkd->bhqd', p_sel, v_sel.astype(jnp.float32)))
        return o.astype(q_c.dtype)

    out = lax.map(step, (jnp.arange(n_chunks), to_chunks(q), to_chunks(sel), to_chunks(sel_valid)))
    out = out.transpose(1, 0, 3, 2, 4).reshape(b, s_pad, h * d)[:, :s]
    return out


def hybrid_layer(x, ffn1_norm, ffn1_w_gate, ffn1_w_up, ffn1_w_down, mix_norm, w_in,
                 conv_dw, conv_dw_bias, conv_ln_gain, conv_ln_bias, w_conv_proj,
                 q_norm, k_norm, w_attn_proj, w_out,
                 ffn2_norm, ffn2_w_gate, ffn2_w_up, ffn2_w_down):
    b, s, _ = x.shape
    x = x + 0.5 * swiglu(rms_norm(x, ffn1_norm), ffn1_w_gate, ffn1_w_up, ffn1_w_down)
    h = rms_norm(x, mix_norm)
    proj = h @ w_in
    conv_a, conv_b, q, k, v, gate_a, gate_b = jnp.split(proj, SPLITS, axis=-1)
    u = conv_a * jax.nn.sigmoid(conv_b)
    u = causal_depthwise_conv(u, conv_dw, conv_dw_bias)
    u = jax.nn.silu(layer_norm(u, conv_ln_gain, conv_ln_bias))
    y_a = u @ w_conv_proj
    q = rms_norm(q.reshape(b, s, N_HEADS, HEAD_DIM), q_norm)
    k = rms_norm(k.reshape(b, s, N_HEADS, HEAD_DIM), k_norm)
    v = v.reshape(b, s, N_HEADS, HEAD_DIM)
    y_b = moba_attention(q, k, v) @ w_attn_proj
    mixed = jax.nn.sigmoid(gate_a) * y_a + jax.nn.sigmoid(gate_b) * y_b
    x = x + mixed @ w_out
    x = x + 0.5 * swiglu(rms_norm(x, ffn2_norm), ffn2_w_gate, ffn2_w_up, ffn2_w_down)
    return x


def setup_inputs(seed: int = 0) -> dict:
    key = jax.random.key(seed)
    ks = jax.random.split(key, 20)
    L = DEPTH
    f32 = jnp.float32

    def normal(k, shape, fan_in):
        return jax.random.normal(k, shape, f32) * (fan_in ** -0.5)

    def gain(k, n):
        return 1.0 + 0.02 * jax.random.normal(k, (L, n), f32)

    def bias(k, n):
        return 0.02 * jax.random.normal(k, (L, n), f32)

    return {
        'x': jax.random.normal(ks[0], (BATCH, SEQ, D_MODEL), f32),
        'ffn1_norm': gain(ks[1], D_MODEL),
        'ffn1_w_gate': normal(ks[2], (L, D_MODEL, D_FF), D_MODEL),
        'ffn1_w_up': normal(ks[3], (L, D_MODEL, D_FF), D_MODEL),
        'ffn1_w_down': normal(ks[4], (L, D_FF, D_MODEL), D_FF),
        'mix_norm': gain(ks[5], D_MODEL),
        'w_in': normal(ks[6], (L, D_MODEL, IN_COLS), D_MODEL),
        'conv_dw': normal(ks[7], (L, CONV_WIDTH, CONV_CHANNELS), CONV_WIDTH),
        'conv_dw_bias': bias(ks[8], CONV_CHANNELS),
        'conv_ln_gain': gain(ks[9], CONV_CHANNELS),
        'conv_ln_bias': bias(ks[10], CONV_CHANNELS),
        'w_conv_proj': normal(ks[11], (L, CONV_CHANNELS, D_MODEL), CONV_CHANNELS),
        'q_norm': gain(ks[12], HEAD_DIM),
        'k_norm': gain(ks[13], HEAD_DIM),
        'w_attn_proj': normal(ks[14], (L, ATTN_WIDTH, D_MODEL), ATTN_WIDTH),
        'w_out': normal(ks[15], (L, D_MODEL, D_MODEL), D_MODEL),
        'ffn2_norm': gain(ks[16], D_MODEL),
        'ffn2_w_gate': normal(ks[17], (L, D_MODEL, D_FF), D_MODEL),
        'ffn2_w_up': normal(ks[18], (L, D_MODEL, D_FF), D_MODEL),
        'ffn2_w_down': normal(ks[19], (L, D_FF, D_MODEL), D_FF),
    }


def reference(x, ffn1_norm, ffn1_w_gate, ffn1_w_up, ffn1_w_down, mix_norm, w_in,
              conv_dw, conv_dw_bias, conv_ln_gain, conv_ln_bias, w_conv_proj,
              q_norm, k_norm, w_attn_proj, w_out,
              ffn2_norm, ffn2_w_gate, ffn2_w_up, ffn2_w_down):
    for i in range(DEPTH):
        x = hybrid_layer(
            x, ffn1_norm[i], ffn1_w_gate[i], ffn1_w_up[i], ffn1_w_down[i], mix_norm[i], w_in[i],
            conv_dw[i], conv_dw_bias[i], conv_ln_gain[i], conv_ln_bias[i], w_conv_proj[i],
            q_norm[i], k_norm[i], w_attn_proj[i], w_out[i],
            ffn2_norm[i], ffn2_w_gate[i], ffn2_w_up[i], ffn2_w_down[i])
    return x
```

```python
import contextlib
import numpy as np
import concourse.bass as bass
import concourse.mybir as mybir
from concourse.bass_utils import run_bass_kernel_spmd

F32 = mybir.dt.float32
BF16 = mybir.dt.bfloat16
ALU = mybir.AluOpType
AF = mybir.ActivationFunctionType

D = 1024
DFF = 2816
NH = 16
HD = 64
CW = 31
BLK = 256
TOPK = 3
EPS = 1e-6
INC = 7168
NCORES = 8
S_HALF = 4096
MASKNEG = -240000.0


class Buf:
    __slots__ = ("w", "r")

    def __init__(self):
        self.w = None
        self.r = {}


class Prog:
    COMPUTE = ("pe", "act", "dve", "pool")

    def __init__(self, nc, stack, n_dma=12):
        self.nc = nc
        self.ops = {e: [] for e in ("pe", "act", "dve", "pool", "sp")}
        self.sig = {e: 0 for e in self.COMPUTE}
        self.seen = {e: {} for e in self.ops}
        self.sems = {}
        for e in self.COMPUTE:
            self.sems[e] = stack.enter_context(nc.semaphore("s_" + e))
        self.dq = {}
        for q in ("sp", "pool"):
            keys = []
            for i in range(n_dma):
                k = "%s_d%d" % (q, i)
                self.sems[k] = stack.enter_context(nc.semaphore(k))
                keys.append(k)
            self.dq[q] = {"keys": keys, "cnt": {k: 0 for k in keys}, "i": 0}

    def _need(self, eng, deps):
        waits = []
        seen = self.seen[eng]
        for (k, v) in deps:
            if eng == "pe" and k == "pe":
                continue
            if seen.get(k, 0) >= v:
                continue
            seen[k] = v
            waits.append((k, v))
        return waits

    def _deps(self, reads, writes):
        deps = set()
        for b in reads:
            if b.w is not None:
                deps.add(b.w)
        for b in writes:
            if b.w is not None:
                deps.add(b.w)
            for k, v in b.r.items():
                deps.add((k, v))
        mx = {}
        for k, v in deps:
            if mx.get(k, 0) < v:
                mx[k] = v
        return list(mx.items())

    def _mark(self, ev, reads, writes):
        k, v = ev
        for b in reads:
            if b.r.get(k, 0) < v:
                b.r[k] = v
        for b in writes:
            b.w = ev
            b.r = {}

    def op(self, eng, fn, reads=(), writes=(), signal=True):
        waits = self._need(eng, self._deps(reads, writes))
        if signal:
            self.sig[eng] += 1
            ev = (eng, self.sig[eng])
        else:
            ev = (eng, self.sig[eng] + 1)
        self.ops[eng].append((waits, fn, (eng, 1) if signal else None))
        self._mark(ev, reads, writes)
        return ev

    def dma(self, q, fn, reads=(), writes=()):
        dq = self.dq[q]
        k = dq["keys"][dq["i"] % len(dq["keys"])]
        dq["i"] += 1
        deps = self._deps(reads, writes)
        if dq["cnt"][k] > 0:
            deps.append((k, dq["cnt"][k]))
        waits = self._need(q, deps)
        dq["cnt"][k] += 16
        ev = (k, dq["cnt"][k])
        self.ops[q].append((waits, fn, (k, 16)))
        self._mark(ev, reads, writes)
        return ev

    def all_events(self):
        evs = [(e, self.sig[e]) for e in self.COMPUTE if self.sig[e] > 0]
        for q in self.dq.values():
            for k, c in q["cnt"].items():
                if c > 0:
                    evs.append((k, c))
        return evs

    def barrier(self):
        evs = self.all_events()
        for eng in self.ops:
            waits = self._need(eng, [e for e in evs if not (e[0] == eng)])
            if waits:
                self.ops[eng].append((waits, None, None))

    def finish(self):
        evs = self.all_events()
        waits = self._need("sp", evs)
        self.ops["sp"].append((waits, None, None))

    def emit(self, block):
        sems = self.sems

        def run(eng_obj, lst):
            for waits, fn, inc in lst:
                for k, v in waits:
                    eng_obj.wait_ge(sems[k], v)
                if fn is not None:
                    ins = fn(eng_obj)
                    if inc is not None:
                        ins.then_inc(sems[inc[0]], inc[1])

        ops = self.ops

        @block.tensor
        def _(e):
            run(e, ops["pe"])

        @block.scalar
        def _(e):
            run(e, ops["act"])

        @block.vector
        def _(e):
            run(e, ops["dve"])

        @block.gpsimd
        def _(e):
            run(e, ops["pool"])

        @block.sync
        def _(e):
            run(e, ops["sp"])


class Ctx:
    pass


def sb(nc, st, name, shape, dt):
    return st.enter_context(nc.sbuf_tensor(name, shape, dt))


def ps(nc, st, name, shape, dt):
    return st.enter_context(nc.psum_tensor(name, shape, dt))


def load_w_bf16(P, dst, dst_buf, src, nchunk):
    step = 8 if nchunk <= 8 else 11
    for k0 in range(0, nchunk, step):
        k1 = min(nchunk, k0 + step)
        P.dma("pool", lambda e, k0=k0, k1=k1: e.dma_start(
            out=dst[:, k0:k1, :], in_=src[k0 * 128:k1 * 128, :].rearrange("(k p) n -> p k n", p=128)),
              writes=[dst_buf])


def make_identity(P, nc, st, c):
    c.idf = sb(nc, st, "idf", [128, 128], F32)
    c.idb = sb(nc, st, "idb", [128, 128], BF16)
    c.idf_b = Buf()
    c.idb_b = Buf()
    tmp = sb(nc, st, "idtmp", [128, 128], F32)
    tb = Buf()
    P.op("pool", lambda e: e.iota(tmp[:], [[1, 128]], base=0, channel_multiplier=-1,
                                  allow_small_or_imprecise_dtypes=True), writes=[tb])
    P.op("pool", lambda e: e.tensor_single_scalar(out=c.idf[:], in_=tmp[:], scalar=0.0, op=ALU.is_equal),
         reads=[tb], writes=[c.idf_b])
    P.op("pool", lambda e: e.tensor_copy(out=c.idb[:], in_=c.idf[:]), reads=[c.idf_b], writes=[c.idb_b])


def norm_pre(P, c, W, xsrc, xsrc_buf):
    i = W.nt_i
    W.nt_i += 1
    j = i % len(W.xn)
    xn, xnb = W.xn[j], W.xn_b[j]
    ssc = W.ss[:, i % 8:i % 8 + 1]
    ssb = W.ss_b[i % 8]
    P.op("act", lambda e: e.activation(out=xn[:], in_=xsrc, func=AF.Square, accum_out=ssc),
         reads=[xsrc_buf], writes=[xnb, ssb])
    msc = W.ms[:, i % 8:i % 8 + 1]
    msb = W.ms_b[i % 8]
    P.op("dve", lambda e: e.tensor_scalar(out=msc, in0=ssc, scalar1=1.0 / D, scalar2=EPS,
                                         op0=ALU.mult, op1=ALU.add), reads=[ssb], writes=[msb])
    rsc = W.rs[:, i % 8:i % 8 + 1]
    rsb = W.rs_b[i % 8]
    P.op("pool", lambda e: e.tensor_tensor(out=rsc, in0=msc, in1=c.mhalf[:, 0:1], op=ALU.pow),
         reads=[msb, c.mhalf_b], writes=[rsb])
    P.op("dve", lambda e: e.tensor_scalar(out=xn[:], in0=xsrc, scalar1=rsc, scalar2=None, op0=ALU.mult),
         reads=[xsrc_buf, rsb], writes=[xnb])
    return xn, xnb


def norm_T(P, c, W, xn, xnb, gcol, gcol_buf, dstT, dstT_buf, s):
    j = W.tp_i % 2
    W.tp_i += 1
    tp, tpb = W.tp[j], W.tp_b[j]
    for k in range(8):
        P.op("pe", lambda e, k=k: e.transpose(out=tp[:, k * 128:(k + 1) * 128], in_=xn[:, k * 128:(k + 1) * 128],
                                              identity=c.idb[:]),
             reads=[xnb, c.idb_b], writes=[tpb], signal=(k == 7))
    P.op("dve", lambda e: e.tensor_tensor(out=dstT[:, :, s * 128:(s + 1) * 128],
                                         in0=tp[:].rearrange("p (k t) -> p k t", t=128),
                                         in1=gcol[:, 0:8].unsqueeze(2).to_broadcast([128, 8, 128]), op=ALU.mult),
         reads=[tpb, gcol_buf], writes=[dstT_buf])


def alloc_norm_ws(nc, st, W, pfx, nxn=4):
    W.nt_i = 0
    W.tp_i = 0
    W.ss = sb(nc, st, pfx + "ss", [128, 8], F32)
    W.ms = sb(nc, st, pfx + "ms", [128, 8], F32)
    W.rs = sb(nc, st, pfx + "rs", [128, 8], F32)
    W.ss_b = [Buf() for _ in range(8)]
    W.ms_b = [Buf() for _ in range(8)]
    W.rs_b = [Buf() for _ in range(8)]
    W.xn = [sb(nc, st, pfx + "xn%d" % j, [128, D], BF16) for j in range(nxn)]
    W.xn_b = [Buf() for _ in range(nxn)]
    W.tp = [ps(nc, st, pfx + "tp%d" % j, [128, D], BF16) for j in range(2)]
    W.tp_b = [Buf() for _ in range(2)]


def load_col(P, nc, st, name, src_row, n):
    t = sb(nc, st, name, [128, n], F32)
    b = Buf()
    P.dma("sp", lambda e: e.dma_start(out=t[:], in_=src_row.rearrange("o (k p) -> p (o k)", p=128),
                                      allow_slow_non_contiguous=True), writes=[b])
    return t, b


def ffn_phase(P, nc, c, pfx, tiles, g_row, wg, wu, wd, epilogue, g2_row=None, tile_end=None):
    with contextlib.ExitStack() as st:
        W = Ctx()
        W.wg = sb(nc, st, pfx + "wg", [128, 8, DFF], BF16)
        W.wu = sb(nc, st, pfx + "wu", [128, 8, DFF], BF16)
        W.wd = sb(nc, st, pfx + "wd", [128, 22, D], BF16)
        W.wd_b = Buf()
        W.wg_b = [Buf(), Buf()]
        W.wu_b = [Buf(), Buf()]
        for half in range(2):
            c0, c1 = half * 1408, (half + 1) * 1408
            for (dst, src, bufs) in ((W.wg, wg, W.wg_b), (W.wu, wu, W.wu_b)):
                P.dma("pool", lambda e, dst=dst, src=src, c0=c0, c1=c1: e.dma_start(
                    out=dst[:, :, c0:c1], in_=src[:, c0:c1].rearrange("(k p) n -> p k n", p=128)), writes=[bufs[half]])
        load_w_bf16(P, W.wd, W.wd_b, wd, 22)

        def fblk(f):
            return 0 if f < 11 else 1
        W.gcol, W.gcol_b = load_col(P, nc, st, pfx + "gcol", g_row, 8)
        if g2_row is not None:
            W.g2col, W.g2col_b = load_col(P, nc, st, pfx + "g2col", g2_row, 8)
        alloc_norm_ws(nc, st, W, pfx, nxn=4)
        NXS = 3
        W.xs = [sb(nc, st, pfx + "xs%d" % j, [128, D], F32) for j in range(NXS)]
        W.xs_b = [Buf() for _ in range(NXS)]
        W.xs_i = 0
        W.hT = [sb(nc, st, pfx + "hT%d" % j, [128, 8, 512], BF16) for j in range(2)]
        W.hT_b = [Buf() for _ in range(2)]
        W.actT = sb(nc, st, pfx + "actT", [128, 22, 512], BF16)
        W.actT_b = [Buf() for _ in range(22)]
        W.sg = [sb(nc, st, pfx + "sg%d" % j, [128, 512], BF16) for j in range(2)]
        W.sg_b = [Buf() for _ in range(2)]
        W.gps = [ps(nc, st, pfx + "gps%d" % j, [128, 512], F32) for j in range(2)]
        W.ups = [ps(nc, st, pfx + "ups%d" % j, [128, 512], F32) for j in range(2)]
        W.gps_b = [Buf() for _ in range(2)]
        W.ups_b = [Buf() for _ in range(2)]
        W.dps = [ps(nc, st, pfx + "dps%d" % j, [128, 512], F32) for j in range(2)]
        W.dps_b = [Buf() for _ in range(2)]
        if g2_row is not None:
            W.h2T = sb(nc, st, pfx + "h2T", [128, 8, 512], BF16)
            W.h2T_b = Buf()

        def load_x(tile, s):
            j = W.xs_i % NXS
            W.xs_i += 1
            src = tile["src"]
            P.dma("sp", lambda e: e.dma_start(out=W.xs[j][:], in_=src[s * 128:(s + 1) * 128, :]),
                  reads=[tile["buf"]] if tile.get("buf") else [], writes=[W.xs_b[j]])
            return W.xs[j], W.xs_b[j]

        def n_pre(tile):
            res = []
            for s in range(4):
                x, xb = load_x(tile, s)
                res.append(norm_pre(P, c, W, x[:], xb))
            return res

        def n_T(t, xns):
            for s in range(4):
                norm_T(P, c, W, xns[s][0], xns[s][1], W.gcol, W.gcol_b, W.hT[t % 2], W.hT_b[t % 2], s)

        xns = n_pre(tiles[0])
        n_T(0, xns)
        for t, tile in enumerate(tiles):
            hT, hTb = W.hT[t % 2], W.hT_b[t % 2]
            nxt = None
            if t + 1 < len(tiles):
                nxt = n_pre(tiles[t + 1])
            for f in range(22):
                j = f % 2
                for k in range(8):
                    P.op("pe", lambda e, f=f, k=k, j=j, hT=hT: e.matmul(out=W.gps[j][:], lhsT=W.wg[:, k, f * 128:(f + 1) * 128],
                                                                 rhs=hT[:, k, :], start=(k == 0), stop=(k == 7)),
                         reads=[W.wg_b[fblk(f)], hTb], writes=[W.gps_b[j]], signal=(k == 7))
                for k in range(8):
                    P.op("pe", lambda e, f=f, k=k, j=j, hT=hT: e.matmul(out=W.ups[j][:], lhsT=W.wu[:, k, f * 128:(f + 1) * 128],
                                                                 rhs=hT[:, k, :], start=(k == 0), stop=(k == 7)),
                         reads=[W.wu_b[fblk(f)], hTb], writes=[W.ups_b[j]], signal=(k == 7))
                P.op("act", lambda e, j=j: e.activation(out=W.sg[j][:], in_=W.gps[j][:], func=AF.Silu),
                     reads=[W.gps_b[j]], writes=[W.sg_b[j]])
                P.op("dve", lambda e, j=j, f=f: e.tensor_tensor(out=W.actT[:, f, :], in0=W.ups[j][:], in1=W.sg[j][:],
                                                                op=ALU.mult),
                     reads=[W.ups_b[j], W.sg_b[j]], writes=[W.actT_b[f]])
            if nxt is not None:
                n_T(t + 1, nxt)
            q = 0
            pend = None
            for s in range(4):
                x, xb = load_x(tile, s)
                for hf in range(2):
                    j = q % 2
                    q += 1
                    for f in range(22):
                        P.op("pe", lambda e, f=f, s=s, hf=hf, j=j: e.matmul(
                            out=W.dps[j][:], lhsT=W.actT[:, f, s * 128:(s + 1) * 128],
                            rhs=W.wd[:, f, hf * 512:(hf + 1) * 512], start=(f == 0), stop=(f == 21)),
                             reads=[W.actT_b[f], W.wd_b], writes=[W.dps_b[j]], signal=(f == 21))
                    P.op("dve", lambda e, x=x, hf=hf, j=j: e.scalar_tensor_tensor(
                        out=x[:, hf * 512:(hf + 1) * 512], in0=W.dps[j][:], scalar=0.5,
                        in1=x[:, hf * 512:(hf + 1) * 512], op0=ALU.mult, op1=ALU.add),
                         reads=[W.dps_b[j], xb], writes=[xb])
                if pend is not None:
                    pend()
                pend = (lambda tile=tile, s=s, x=x, xb=xb: epilogue(W, tile, s, x, xb))
            pend()
            pend = None
            if tile_end is not None:
                tile_end(W, tile)
        P.barrier()


def mm_group(P, out, out_b, pairs, reads):
    n = len(pairs)
    for i, (l, r) in enumerate(pairs):
        P.op("pe", lambda e, l=l, r=r, i=i: e.matmul(out=out, lhsT=l, rhs=r, start=(i == 0), stop=(i == n - 1)),
             reads=reads, writes=[out_b], signal=(i == n - 1))


def head_norm(P, c, W, src_ps, src_b, gbc, gbc_b, dst32, dst32_b, hf):
    i = W.hn_i
    W.hn_i += 1
    r = i % len(W.raw)
    raw, rawb = W.raw[r], W.raw_b[r]
    P.op("act", lambda e: e.activation(out=raw[:], in_=src_ps, func=AF.Copy), reads=[src_b], writes=[rawb])
    j = i % len(W.sq)
    sq, sqb = W.sq[j], W.sq_b[j]
    P.op("act", lambda e: e.activation(out=sq[:], in_=raw[:], func=AF.Square), reads=[rawb], writes=[sqb])
    hs, hsb = W.hs[j], W.hs_b[j]
    P.op("dve", lambda e: e.tensor_reduce(out=hs[:, 0:8], in_=sq[:].rearrange("p (h d) -> p h d", d=HD),
                                         axis=mybir.AxisListType.X, op=ALU.add), reads=[sqb], writes=[hsb])
    P.op("dve", lambda e: e.tensor_scalar(out=hs[:, 8:16], in0=hs[:, 0:8], scalar1=1.0 / HD, scalar2=EPS,
                                         op0=ALU.mult, op1=ALU.add), reads=[hsb], writes=[hsb])
    P.op("pool", lambda e: e.tensor_tensor(out=hs[:, 16:24], in0=hs[:, 8:16], in1=c.mhalf[:, 0:8], op=ALU.pow),
         reads=[hsb, c.mhalf_b], writes=[hsb])
    P.op("dve", lambda e: e.tensor_tensor(out=sq[:].rearrange("p (h d) -> p h d", d=HD),
                                         in0=raw[:].rearrange("p (h d) -> p h d", d=HD),
                                         in1=hs[:, 16:24].unsqueeze(2).to_broadcast([128, 8, HD]), op=ALU.mult),
         reads=[rawb, hsb], writes=[sqb])
    P.op("pool", lambda e: e.tensor_tensor(out=dst32[:, hf * 512:(hf + 1) * 512].rearrange("p (h d) -> p h d", d=HD),
                                          in0=sq[:].rearrange("p (h d) -> p h d", d=HD),
                                          in1=gbc[:].unsqueeze(1).to_broadcast([128, 8, HD]), op=ALU.mult),
         reads=[sqb, gbc_b], writes=[dst32_b])


def build(nc, s_half=S_HALF, phases=("ffn1", "proj", "conv", "attn", "ffn2"), debug=()):
    NT = s_half // 512
    NB = s_half // BLK
    NTOK = 2 * s_half
    NCH = NTOK // 128
    ins = {}

    def din(name, shape):
        ins[name] = nc.dram_tensor(name, shape, F32, kind="ExternalInput").ap()
        return ins[name]

    x_own = din("x_own", [s_half, D])
    x_prev = din("x_prev", [s_half, D])
    slot_bias = din("slot_bias", [1, 32])
    ffn1_norm = din("ffn1_norm", [1, D])
    ffn1_wg = din("ffn1_w_gate", [D, DFF])
    ffn1_wu = din("ffn1_w_up", [D, DFF])
    ffn1_wd = din("ffn1_w_down", [DFF, D])
    mix_norm = din("mix_norm", [1, D])
    w_in = din("w_in", [D, INC])
    conv_dw = din("conv_dw", [CW, D])
    conv_dw_bias = din("conv_dw_bias", [1, D])
    conv_ln_gain = din("conv_ln_gain", [1, D])
    conv_ln_bias = din("conv_ln_bias", [1, D])
    w_conv_proj = din("w_conv_proj", [D, D])
    q_norm = din("q_norm", [1, HD])
    k_norm = din("k_norm", [1, HD])
    w_attn_proj = din("w_attn_proj", [D, D])
    w_out = din("w_out", [D, D])
    ffn2_norm = din("ffn2_norm", [1, D])
    ffn2_wg = din("ffn2_w_gate", [D, DFF])
    ffn2_wu = din("ffn2_w_up", [D, DFF])
    ffn2_wd = din("ffn2_w_down", [DFF, D])
    out = nc.dram_tensor("out", [s_half, D], F32, kind="ExternalOutput").ap()

    def scratch(name, shape, dt):
        if name in debug:
            return nc.dram_tensor("dbg_" + name, shape, dt, kind="ExternalOutput").ap()
        return nc.dram_tensor(name, shape, dt, kind="Internal").ap()

    x1_d = scratch("x1_d", [s_half, D], F32)
    x2_d = scratch("x2_d", [s_half, D], F32)
    h2T_d = scratch("h2T_d", [8, 128, NTOK], BF16)
    kT_d = scratch("kT_d", [NH, HD, NTOK], BF16)
    v_d = scratch("v_d", [NH, 128, NCH, HD + 1], BF16)
    qaT_d = scratch("qaT_d", [NH, 96, s_half], BF16)
    uT_d = scratch("uT_d", [8, 128, NB * 288], BF16)
    gaT_d = scratch("gaT_d", [8, 128, s_half], F32)
    gbT_d = scratch("gbT_d", [8, 128, s_half], F32)
    maT_d = scratch("maT_d", [8, 128, s_half], F32)
    x1_b = [Buf() for _ in range(NT)]
    x2_b = [Buf() for _ in range(NT)]
    h2T_b = [Buf() for _ in range(2 * NT)]
    kv_b = [Buf() for _ in range(2 * NT)]
    qa_b = [Buf() for _ in range(NT)]
    u_b = [Buf() for _ in range(2 * NT)]
    g_b = [Buf() for _ in range(NT)]
    ma_b = [Buf() for _ in range(NT)]

    with contextlib.ExitStack() as st:
        P = Prog(nc, st)
        block = st.enter_context(nc.Block())
        c = Ctx()
        make_identity(P, nc, st, c)
        c.mhalf = sb(nc, st, "mhalf", [128, 16], F32)
        c.mhalf_b = Buf()
        P.op("pool", lambda e: e.memset(c.mhalf[:], -0.5), writes=[c.mhalf_b])
        c.kmbd = sb(nc, st, "kmbd", [128, 8, 64], F32)
        c.kmbd_b = Buf()
        P.op("pool", lambda e: e.memset(c.kmbd[:], 0.0), writes=[c.kmbd_b])

        if "ffn1" in phases:
            tiles = []
            for t in range(NT):
                tiles.append(dict(src=x_prev[t * 512:(t + 1) * 512, :], own=False, idx=t))
            for t in range(NT):
                tiles.append(dict(src=x_own[t * 512:(t + 1) * 512, :], own=True, idx=t))

            def epi1(W, tile, s, x, xb):
                if tile["own"]:
                    i = tile["idx"]
                    P.dma("sp", lambda e: e.dma_start(out=x1_d[i * 512 + s * 128:i * 512 + (s + 1) * 128, :], in_=x[:]),
                          reads=[xb], writes=[x1_b[i]])
                xn, xnb = norm_pre(P, c, W, x[:], xb)
                norm_T(P, c, W, xn, xnb, W.g2col, W.g2col_b, W.h2T, W.h2T_b, s)

            def end1(W, tile):
                gi = tile["idx"] + (NT if tile["own"] else 0)
                P.dma("sp", lambda e: e.dma_start(
                    out=h2T_d[:, :, gi * 512:(gi + 1) * 512].rearrange("k p t -> p k t"), in_=W.h2T[:]),
                      reads=[W.h2T_b], writes=[h2T_b[gi]])

            ffn_phase(P, nc, c, "f1", tiles, ffn1_norm, ffn1_wg, ffn1_wu, ffn1_wd, epi1, g2_row=mix_norm, tile_end=end1)

        if "proj" in phases:
            with contextlib.ExitStack() as s2:
                W2 = Ctx()
                W2.w = sb(nc, s2, "p2w", [128, 8, INC], BF16)
                _wkv, _wq, _wcv, _wgt = Buf(), Buf(), Buf(), Buf()
                W2.w_b = {"kv": _wkv, "q": _wq, "conv": _wcv, "gate": _wgt}
                for (c0, c1, bl) in ((3072, 5120, [_wkv]), (0, 3072, [_wq, _wcv]), (5120, 7168, [_wgt])):
                    P.dma("pool", lambda e, c0=c0, c1=c1: e.dma_start(
                        out=W2.w[:, :, c0:c1], in_=w_in[:, c0:c1].rearrange("(k p) n -> p k n", p=128)), writes=bl)

                def wbuf(col0):
                    return W2.w_b["conv" if col0 < 2048 else "q" if col0 < 3072 else "kv" if col0 < 5120 else "gate"]
                W2.gq = sb(nc, s2, "p2gq", [128, HD], F32)
                W2.gk = sb(nc, s2, "p2gk", [128, HD], F32)
                W2.gq_b, W2.gk_b = Buf(), Buf()
                P.dma("sp", lambda e: e.dma_start(out=W2.gq[:], in_=q_norm[0:1, :].to_broadcast([128, HD])), writes=[W2.gq_b])
                P.dma("sp", lambda e: e.dma_start(out=W2.gk[:], in_=k_norm[0:1, :].to_broadcast([128, HD])), writes=[W2.gk_b])
                W2.sbias = sb(nc, s2, "p2sbias", [128, 32], F32)
                W2.sbias_b = Buf()
                P.dma("sp", lambda e: e.dma_start(out=W2.sbias[:], in_=slot_bias[0:1, :].to_broadcast([128, 32])),
                      writes=[W2.sbias_b])
                W2.sbj = sb(nc, s2, "p2sbj", [128, 32], F32)
                W2.sbj_b = Buf()
                W2.ones32 = sb(nc, s2, "p2ones", [128, 1], F32)
                W2.ones32_b = Buf()
                P.op("pool", lambda e: e.memset(W2.ones32[:], 1.0), writes=[W2.ones32_b])
                W2.hT = [sb(nc, s2, "p2hT%d" % j, [128, 8, 512], BF16) for j in range(2)]
                W2.hT_b = [Buf() for _ in range(2)]
                W2.raw = [sb(nc, s2, "p2raw%d" % j, [128, 512], F32) for j in range(4)]
                W2.raw_b = [Buf() for _ in range(4)]
                W2.mm = [ps(nc, s2, "p2mm%d" % j, [128, 512], F32) for j in range(3)]
                W2.mm_b = [Buf() for _ in range(3)]
                W2.mm_i = 0
                W2.tp = ps(nc, s2, "p2tp", [128, 1024], BF16)
                W2.tp_b = Buf()
                W2.tq = ps(nc, s2, "p2tq", [128, 1024], F32)
                W2.tq_b = Buf()
                W2.gps = ps(nc, s2, "p2gps", [128, 512], F32)
                W2.gps_b = Buf()
                W2.kmps = ps(nc, s2, "p2kmps", [128, 16], F32)
                W2.kmps_b = Buf()
                W2.hn_i = 0
                W2.sq = [sb(nc, s2, "p2sq%d" % j, [128, 512], F32) for j in range(4)]
                W2.sq_b = [Buf() for _ in range(4)]
                W2.hs = [sb(nc, s2, "p2hs%d" % j, [128, 24], F32) for j in range(4)]
                W2.hs_b = [Buf() for _ in range(4)]
                W2.kn32 = [sb(nc, s2, "p2kn%d" % j, [128, D], F32) for j in range(2)]
                W2.kn32_b = [Buf() for _ in range(2)]
                W2.knb = sb(nc, s2, "p2knb", [128, D], BF16)
                W2.knb_b = Buf()
                W2.kT = sb(nc, s2, "p2kT", [128, 8, 128], BF16)
                W2.kT_b = Buf()
                W2.va = [sb(nc, s2, "p2va%d" % j, [128, NH, HD + 1], BF16) for j in range(2)]
                W2.va_b = [Buf() for _ in range(2)]
                for j in range(2):
                    P.op("pool", lambda e, j=j: e.memset(W2.va[j][:], 1.0), writes=[W2.va_b[j]])
                W2.qn32 = [sb(nc, s2, "p2qn%d" % j, [128, D], F32) for j in range(2)]
                W2.qn32_b = [Buf() for _ in range(2)]
                W2.qT32 = sb(nc, s2, "p2qT32", [128, 8, 128], F32)
                W2.qT32_b = Buf()
                W2.gate = sb(nc, s2, "p2gate", [128, NH, 32], F32)
                W2.gate_b = Buf()
                W2.top8 = sb(nc, s2, "p2top8", [128, NH, 8], F32)
                W2.top8_b = Buf()
                W2.sel = sb(nc, s2, "p2sel", [128, NH, 32], F32)
                W2.sel_b = Buf()
                W2.val = sb(nc, s2, "p2val", [128, NH, 32], F32)
                W2.val_b = Buf()
                W2.qa = sb(nc, s2, "p2qa", [128, NH, 96], BF16)
                W2.qa_b = Buf()
                W2.qaT = sb(nc, s2, "p2qaT", [96, NH, 128], BF16)
                W2.qaT_b = Buf()
                W2.ev = [sb(nc, s2, "p2ev%d" % j, [128, 512], F32) for j in range(2)]
                W2.ev_b = [Buf() for _ in range(2)]
                W2.uo = [sb(nc, s2, "p2uo%d" % j, [128, 512], BF16) for j in range(2)]
                W2.uo_b = [Buf() for _ in range(2)]
                W2.go = [sb(nc, s2, "p2go%d" % j, [128, 512], F32) for j in range(2)]
                W2.go_b = [Buf() for _ in range(2)]

                def nextmm():
                    j = W2.mm_i % 3
                    W2.mm_i += 1
                    return W2.mm[j], W2.mm_b[j]

                def tok_proj(hT, hTb, s, col0):
                    pt, pb = nextmm()
                    mm_group(P, pt[:], pb, [(hT[:, k, s * 128:(s + 1) * 128], W2.w[:, k, col0:col0 + 512])
                                            for k in range(8)], [hTb, wbuf(col0)])
                    return pt, pb

                def feat_proj(hT, hTb, col0):
                    pt, pb = nextmm()
                    mm_group(P, pt[:], pb, [(W2.w[:, k, col0:col0 + 128], hT[:, k, :]) for k in range(8)],
                             [hTb, wbuf(col0)])
                    return pt, pb

                def load_hT(gi):
                    hT, hTb = W2.hT[gi % 2], W2.hT_b[gi % 2]
                    P.dma("sp", lambda e: e.dma_start(
                        out=hT[:], in_=h2T_d[:, :, gi * 512:(gi + 1) * 512].rearrange("k p t -> p k t")),
                          reads=[h2T_b[gi]], writes=[hTb])

                def chunk_of(gi, s):
                    own = gi >= NT
                    j = 2 * (gi - NT if own else gi) + s // 2
                    return 4 * j + (2 if own else 0) + s % 2

                def stageA(gi, s):
                    own = gi >= NT
                    hT, hTb = W2.hT[gi % 2], W2.hT_b[gi % 2]
                    ch = chunk_of(gi, s)
                    par = ch % 2
                    kn, knb_ = W2.kn32[par], W2.kn32_b[par]
                    for hf in range(2):
                        pt, pb = tok_proj(hT, hTb, s, 3072 + hf * 512)
                        head_norm(P, c, W2, pt[:], pb, W2.gk, W2.gk_b, kn, knb_, hf)
                    va, vab = W2.va[par], W2.va_b[par]
                    for hf in range(2):
                        pt, pb = tok_proj(hT, hTb, s, 4096 + hf * 512)
                        P.op("act", lambda e, pt=pt, hf=hf: e.activation(
                            out=va[:, hf * 8:(hf + 1) * 8, 0:HD], in_=pt[:].rearrange("p (h d) -> p h d", d=HD),
                            func=AF.Copy), reads=[pb], writes=[vab])
                    P.dma("sp", lambda e: e.dma_start(out=v_d[:, :, ch, :].rearrange("h p d -> p h d"), in_=va[:]),
                          reads=[vab], writes=[kv_b[gi]])
                    if own:
                        qn, qnb = W2.qn32[par], W2.qn32_b[par]
                        for hf in range(2):
                            pt, pb = tok_proj(hT, hTb, s, 2048 + hf * 512)
                            head_norm(P, c, W2, pt[:], pb, W2.gq, W2.gq_b, qn, qnb, hf)

                def stageB(gi, s):
                    own = gi >= NT
                    ti = gi - NT
                    ch = chunk_of(gi, s)
                    slot = ch // 2
                    par = ch % 2
                    kn, knb_ = W2.kn32[par], W2.kn32_b[par]
                    P.op("act", lambda e: e.activation(out=W2.knb[:], in_=kn[:], func=AF.Copy),
                         reads=[knb_], writes=[W2.knb_b])
                    for k in range(8):
                        P.op("pe", lambda e, k=k: e.transpose(out=W2.tp[:, k * 128:(k + 1) * 128],
                                                              in_=W2.knb[:, k * 128:(k + 1) * 128], identity=c.idb[:]),
                             reads=[W2.knb_b, c.idb_b], writes=[W2.tp_b], signal=(k == 7))
                    P.op("dve", lambda e: e.tensor_copy(out=W2.kT[:].rearrange("p k t -> p (k t)"), in_=W2.tp[:]),
                         reads=[W2.tp_b], writes=[W2.kT_b])
                    for a in range(2):
                        P.dma("sp", lambda e, a=a: e.dma_start(
                            out=kT_d[a::2, :, ch * 128:(ch + 1) * 128].rearrange("h d t -> d h t"),
                            in_=W2.kT[a * 64:(a + 1) * 64, :, :]), reads=[W2.kT_b], writes=[kv_b[gi]])
                    for k in range(8):
                        mm_group(P, W2.kmps[:, par * 8 + k:par * 8 + k + 1], W2.kmps_b,
                                 [(kn[:, k * 128:(k + 1) * 128], W2.ones32[:, 0:1])], [knb_, W2.ones32_b])
                    if par == 1:
                        for a in range(2):
                            dst = c.kmbd[a * 64:(a + 1) * 64, :, a * 32 + slot]
                            P.op("act", lambda e, a=a, dst=dst: e.activation(
                                out=dst, in_=W2.kmps[a * 64:(a + 1) * 64, 0:8], func=AF.Copy, scale=1.0 / BLK),
                                 reads=[W2.kmps_b], writes=[c.kmbd_b])
                            P.op("dve", lambda e, a=a, dst=dst: e.scalar_tensor_tensor(
                                out=dst, in0=W2.kmps[a * 64:(a + 1) * 64, 8:16], scalar=1.0 / BLK, in1=dst,
                                op0=ALU.mult, op1=ALU.add), reads=[W2.kmps_b, c.kmbd_b], writes=[c.kmbd_b])
                    if not own:
                        yield
                        return
                    qn, qnb = W2.qn32[par], W2.qn32_b[par]
                    jb = (ti * 4 + s) // 2
                    if (ti * 4 + s) % 2 == 0:
                        P.op("pool", lambda e: e.tensor_copy(out=W2.sbj[:], in_=W2.sbias[:]),
                             reads=[W2.sbias_b], writes=[W2.sbj_b])
                        P.op("pool", lambda e: e.memset(W2.sbj[:, 2 * jb + 1:32], -1e30), writes=[W2.sbj_b])
                    for kk in range(8):
                        P.op("pe", lambda e, kk=kk: e.transpose(
                            out=W2.tq[:, kk * 128:(kk + 1) * 128], in_=qn[:, kk * 128:(kk + 1) * 128],
                            identity=c.idf[:]), reads=[qnb, c.idf_b], writes=[W2.tq_b], signal=(kk == 7))
                    P.op("dve", lambda e: e.tensor_copy(
                        out=W2.qT32[:, 0:4, :].rearrange("p k t -> p (k t)"), in_=W2.tq[:, 0:512]),
                         reads=[W2.tq_b], writes=[W2.qT32_b])
                    P.op("act", lambda e: e.activation(
                        out=W2.qT32[:, 4:8, :].rearrange("p k t -> p (k t)"), in_=W2.tq[:, 512:1024], func=AF.Copy),
                         reads=[W2.tq_b], writes=[W2.qT32_b])
                    yield
                    for k in range(8):
                        P.op("pe", lambda e, k=k: e.matmul(out=W2.gps[:, k * 64:(k + 1) * 64], lhsT=W2.qT32[:, k, :],
                                                           rhs=c.kmbd[:, k, :], start=True, stop=True),
                             reads=[W2.qT32_b, c.kmbd_b], writes=[W2.gps_b], signal=(k == 7))
                    P.op("dve", lambda e: e.tensor_tensor(
                        out=W2.gate[:], in0=W2.gps[:].rearrange("p (h n) -> p h n", n=32),
                        in1=W2.sbj[:].unsqueeze(1).to_broadcast([128, NH, 32]), op=ALU.add),
                         reads=[W2.gps_b, W2.sbj_b], writes=[W2.gate_b])
                    for h in range(NH):
                        P.op("dve", lambda e, h=h: e.max(out=W2.top8[:, h, :], in_=W2.gate[:, h, :]),
                             reads=[W2.gate_b], writes=[W2.top8_b])
                    P.op("dve", lambda e: e.tensor_tensor(
                        out=W2.sel[:], in0=W2.gate[:], in1=W2.top8[:, :, TOPK - 1:TOPK].to_broadcast([128, NH, 32]),
                        op=ALU.is_ge), reads=[W2.gate_b, W2.top8_b], writes=[W2.sel_b])
                    P.op("dve", lambda e: e.tensor_single_scalar(out=W2.val[:], in_=W2.gate[:], scalar=-1e29,
                                                                 op=ALU.is_gt), reads=[W2.gate_b], writes=[W2.val_b])
                    P.op("dve", lambda e: e.tensor_tensor(out=W2.sel[:], in0=W2.sel[:], in1=W2.val[:], op=ALU.mult),
                         reads=[W2.sel_b, W2.val_b], writes=[W2.sel_b])
                    P.op("dve", lambda e: e.tensor_scalar(out=W2.qa[:, :, 64:96], in0=W2.sel[:], scalar1=-MASKNEG,
                                                         scalar2=MASKNEG, op0=ALU.mult, op1=ALU.add),
                         reads=[W2.sel_b], writes=[W2.qa_b])
                    P.op("dve", lambda e: e.memset(W2.qa[:, :, 64 + 2 * jb + 1:64 + 2 * jb + 2], 0.0),
                         writes=[W2.qa_b])
                    P.op("act", lambda e: e.activation(out=W2.qa[:, :, 0:64],
                                                       in_=qn[:].rearrange("p (h d) -> p h d", d=HD),
                                                       func=AF.Copy), reads=[qnb], writes=[W2.qa_b])
                    yield
                    for half in range(2):
                        for k in range(8):
                            h = half * 8 + k
                            P.op("pe", lambda e, k=k, h=h: e.transpose(out=W2.tp[0:96, k * 128:(k + 1) * 128],
                                                                       in_=W2.qa[:, h, :], identity=c.idb[:]),
                                 reads=[W2.qa_b, c.idb_b], writes=[W2.tp_b], signal=(k == 7))
                        P.op("dve", lambda e, half=half: e.tensor_copy(
                            out=W2.qaT[:, half * 8:(half + 1) * 8, :].rearrange("p k t -> p (k t)"), in_=W2.tp[0:96, :]),
                             reads=[W2.tp_b], writes=[W2.qaT_b])
                    tok0 = ti * 512 + s * 128
                    P.dma("sp", lambda e: e.dma_start(
                        out=qaT_d[:, :, tok0:tok0 + 128].rearrange("h r t -> r h t"), in_=W2.qaT[:]),
                          reads=[W2.qaT_b], writes=[qa_b[ti]])

                def stageF_list(gi, part):
                    own = gi >= NT
                    ti = gi - NT
                    hT, hTb = W2.hT[gi % 2], W2.hT_b[gi % 2]
                    fl = []

                    def conv_group(k):
                        pa, pab = feat_proj(hT, hTb, k * 128)
                        pbm, pbb = feat_proj(hT, hTb, 1024 + k * 128)
                        j = k % 2
                        P.op("act", lambda e: e.activation(out=W2.ev[j][:], in_=pbm[:], func=AF.Sigmoid),
                             reads=[pbb], writes=[W2.ev_b[j]])
                        P.op("dve", lambda e: e.tensor_tensor(out=W2.uo[j][:], in0=pa[:], in1=W2.ev[j][:], op=ALU.mult),
                             reads=[pab, W2.ev_b[j]], writes=[W2.uo_b[j]])
                        for bb in range(2):
                            jblk = 2 * (ti if own else gi) + bb
                            if own:
                                P.dma("sp", lambda e, bb=bb, jblk=jblk: e.dma_start(
                                    out=uT_d[k, :, jblk * 288 + 32:jblk * 288 + 288], in_=W2.uo[j][:, bb * 256:(bb + 1) * 256]),
                                      reads=[W2.uo_b[j]], writes=[u_b[ti]])
                            else:
                                P.dma("sp", lambda e, bb=bb, jblk=jblk: e.dma_start(
                                    out=uT_d[k, :, jblk * 288:jblk * 288 + 32], in_=W2.uo[j][:, bb * 256 + 224:bb * 256 + 256]),
                                      reads=[W2.uo_b[j]], writes=[u_b[NT + gi]])

                    def gate_group(gidx):
                        which, k = gidx // 8, gidx % 8
                        dst = gaT_d if which == 0 else gbT_d
                        pg, pgb = feat_proj(hT, hTb, 5120 + which * 1024 + k * 128)
                        j = k % 2
                        P.op("act", lambda e: e.activation(out=W2.go[j][:], in_=pg[:], func=AF.Sigmoid),
                             reads=[pgb], writes=[W2.go_b[j]])
                        P.dma("sp", lambda e: e.dma_start(out=dst[k, :, ti * 512:(ti + 1) * 512], in_=W2.go[j][:]),
                              reads=[W2.go_b[j]], writes=[g_b[ti]])

                    for k in (2 * part, 2 * part + 1):
                        fl.append(lambda k=k: conv_group(k))
                    if own:
                        for q4 in range(4):
                            fl.append(lambda g=part * 4 + q4: gate_group(g))
                    return fl

                seq = [(gi, s) for gi in range(2 * NT) for s in range(4)]
                load_hT(0)
                stageA(*seq[0])
                for idx, (gi, s) in enumerate(seq):
                    if s == 0 and gi + 1 < 2 * NT:
                        load_hT(gi + 1)
                    fl = stageF_list(gi, s)
                    nf = len(fl)
                    cuts = [0, (nf + 2) // 3, (2 * nf + 2) // 3, nf]
                    gen = stageB(gi, s)
                    next(gen, None)
                    for f in fl[cuts[0]:cuts[1]]:
                        f()
                    next(gen, None)
                    if idx + 1 < len(seq):
                        stageA(*seq[idx + 1])
                    for f in fl[cuts[1]:cuts[3]]:
                        f()
                    for _ in gen:
                        pass
                P.barrier()

        if "conv" in phases:
            with contextlib.ExitStack() as s3:
                W3 = Ctx()
                W3.wcp = sb(nc, s3, "p3wcp", [128, 8, D], BF16)
                W3.wcp_b = Buf()
                load_w_bf16(P, W3.wcp, W3.wcp_b, w_conv_proj, 8)
                W3.wc = sb(nc, s3, "p3wc", [128, 8, CW], F32)
                W3.wc_b = Buf()
                W3.wraw = sb(nc, s3, "p3wraw", [CW, D], F32)
                W3.wraw_b = Buf()
                P.dma("sp", lambda e: e.dma_start(out=W3.wraw[:], in_=conv_dw[:, :]), writes=[W3.wraw_b])
                W3.wtp = ps(nc, s3, "p3wtp", [128, 8, 32], F32)
                W3.wtp_b = Buf()
                for k in range(8):
                    P.op("pe", lambda e, k=k: e.transpose(out=W3.wtp[:, k, 0:CW], in_=W3.wraw[:, k * 128:(k + 1) * 128],
                                                          identity=c.idf[0:CW, 0:CW]),
                         reads=[W3.wraw_b, c.idf_b], writes=[W3.wtp_b], signal=(k == 7))
                P.op("dve", lambda e: e.tensor_copy(out=W3.wc[:], in_=W3.wtp[:, :, 0:CW]), reads=[W3.wtp_b], writes=[W3.wc_b])
                W3.bcol, W3.bcol_b = load_col(P, nc, s3, "p3bcol", conv_dw_bias, 8)
                W3.lg, W3.lg_b = load_col(P, nc, s3, "p3lg", conv_ln_gain, 8)
                W3.lb, W3.lb_b = load_col(P, nc, s3, "p3lb", conv_ln_bias, 8)
                W3.diag = sb(nc, s3, "p3diag", [128, 8 * CW, 128], BF16)
                W3.diag_b = Buf()
                for k in range(8):
                    eng = "pool" if k in (3, 7) else "dve"
                    P.op(eng, lambda e, k=k: e.tensor_tensor(
                        out=W3.diag[:, k * CW:(k + 1) * CW, :], in0=c.idf[:].unsqueeze(1).to_broadcast([128, CW, 128]),
                        in1=W3.wc[:, k, :].unsqueeze(2).to_broadcast([128, CW, 128]), op=ALU.mult),
                         reads=[c.idf_b, W3.wc_b], writes=[W3.diag_b])
                W3.onesm = sb(nc, s3, "p3onesm", [128, 128], BF16)
                W3.onesm_b = Buf()
                P.op("pool", lambda e: e.memset(W3.onesm[:], 1.0 / D), writes=[W3.onesm_b])
                W3.u = [sb(nc, s3, "p3u%d" % j, [128, 8, 576], BF16) for j in range(2)]
                W3.u_b = [Buf() for _ in range(2)]
                W3.y32 = [sb(nc, s3, "p3y32%d" % j, [128, 8, 512], F32) for j in range(2)]
                W3.y32_b = [[Buf() for _ in range(8)] for _ in range(2)]
                W3.ybf = [sb(nc, s3, "p3ybf%d" % j, [128, 8, 512], BF16) for j in range(2)]
                W3.ybf_b = [[Buf() for _ in range(8)] for _ in range(2)]
                W3.ysq = [sb(nc, s3, "p3ysq%d" % j, [128, 8, 512], BF16) for j in range(2)]
                W3.ysq_b = [[Buf() for _ in range(8)] for _ in range(2)]
                W3.mean = sb(nc, s3, "p3mean", [128, 512], F32)
                W3.mean_b = Buf()
                W3.m2 = sb(nc, s3, "p3m2", [128, 512], F32)
                W3.m2_b = Buf()
                W3.rstd = sb(nc, s3, "p3rstd", [128, 512], F32)
                W3.rstd_b = Buf()
                W3.d1 = [sb(nc, s3, "p3d1%d" % j, [128, 512], F32) for j in range(2)]
                W3.d1_b = [Buf() for _ in range(2)]
                W3.sT = sb(nc, s3, "p3sT", [128, 8, 512], BF16)
                W3.sT_b = [Buf() for _ in range(8)]
                W3.ga = sb(nc, s3, "p3ga", [128, 8, 512], F32)
                W3.ga_b = Buf()
                W3.ma = [sb(nc, s3, "p3ma%d" % j, [128, 512], F32) for j in range(2)]
                W3.ma_b = [Buf() for _ in range(2)]
                W3.cps = [ps(nc, s3, "p3cps%d" % j, [128, 512], F32) for j in range(2)]
                W3.cps_b = [Buf() for _ in range(2)]
                W3.mps = ps(nc, s3, "p3mps", [128, 512], F32)
                W3.mps_b = Buf()
                W3.qps = ps(nc, s3, "p3qps", [128, 512], F32)
                W3.qps_b = Buf()
                W3.yps = [ps(nc, s3, "p3yps%d" % j, [128, 512], F32) for j in range(2)]
                W3.yps_b = [Buf() for _ in range(2)]

                def conv_stage(ti):
                    tb = ti % 2
                    u, ub = W3.u[tb], W3.u_b[tb]
                    P.dma("sp", lambda e: e.dma_start(
                        out=u[:], in_=uT_d[:, :, ti * 576:(ti + 1) * 576].rearrange("k p t -> p k t")),
                          reads=[u_b[ti], u_b[NT + ti]], writes=[ub])
                    for k in range(8):
                        j = k % 2
                        for bb in range(2):
                            mm_group(P, W3.cps[j][:, bb * 256:(bb + 1) * 256], W3.cps_b[j],
                                     [(W3.diag[:, k * CW + jj, :], u[:, k, bb * 288 + 2 + jj:bb * 288 + 2 + jj + 256])
                                      for jj in range(CW)], [W3.diag_b, ub])
                        P.op("act", lambda e, k=k, j=j: e.activation(out=W3.y32[tb][:, k, :], in_=W3.cps[j][:], func=AF.Identity,
                                                                     bias=W3.bcol[:, k:k + 1]),
                             reads=[W3.cps_b[j], W3.bcol_b], writes=[W3.y32_b[tb][k]])
                        P.op("act", lambda e, k=k, j=j: e.activation(out=W3.ybf[tb][:, k, :], in_=W3.cps[j][:], func=AF.Identity,
                                                                     bias=W3.bcol[:, k:k + 1]),
                             reads=[W3.cps_b[j], W3.bcol_b], writes=[W3.ybf_b[tb][k]])
                        P.op("act", lambda e, k=k, j=j: e.activation(out=W3.ysq[tb][:, k, :], in_=W3.cps[j][:], func=AF.Square,
                                                                     bias=W3.bcol[:, k:k + 1]),
                             reads=[W3.cps_b[j], W3.bcol_b], writes=[W3.ysq_b[tb][k]])

                def ln_stage(ti):
                    tb = ti % 2
                    P.dma("sp", lambda e: e.dma_start(
                        out=W3.ga[:], in_=gaT_d[:, :, ti * 512:(ti + 1) * 512].rearrange("k p t -> p k t")),
                          reads=[g_b[ti]], writes=[W3.ga_b])
                    mm_group(P, W3.mps[:], W3.mps_b, [(W3.onesm[:], W3.ybf[tb][:, k, :]) for k in range(8)],
                             [W3.onesm_b] + W3.ybf_b[tb])
                    mm_group(P, W3.qps[:], W3.qps_b, [(W3.onesm[:], W3.ysq[tb][:, k, :]) for k in range(8)],
                             [W3.onesm_b] + W3.ysq_b[tb])
                    P.op("act", lambda e: e.activation(out=W3.mean[:], in_=W3.mps[:], func=AF.Copy),
                         reads=[W3.mps_b], writes=[W3.mean_b])
                    P.op("dve", lambda e: e.tensor_tensor(out=W3.m2[:], in0=W3.mean[:], in1=W3.mean[:], op=ALU.mult),
                         reads=[W3.mean_b], writes=[W3.m2_b])
                    P.op("dve", lambda e: e.scalar_tensor_tensor(out=W3.m2[:], in0=W3.qps[:], scalar=EPS, in1=W3.m2[:],
                                                                 op0=ALU.add, op1=ALU.subtract),
                         reads=[W3.qps_b, W3.m2_b], writes=[W3.m2_b])
                    P.op("act", lambda e: e.activation(out=W3.m2[:], in_=W3.m2[:], func=AF.Sqrt),
                         reads=[W3.m2_b], writes=[W3.m2_b])
                    P.op("dve", lambda e: e.reciprocal(out=W3.rstd[:], in_=W3.m2[:]),
                         reads=[W3.m2_b], writes=[W3.rstd_b])
                    for k in range(8):
                        j = k % 2
                        P.op("dve", lambda e, k=k, j=j: e.tensor_tensor(out=W3.d1[j][:], in0=W3.y32[tb][:, k, :], in1=W3.mean[:],
                                                                        op=ALU.subtract),
                             reads=[W3.y32_b[tb][k], W3.mean_b], writes=[W3.d1_b[j]])
                        P.op("pool", lambda e, j=j: e.tensor_tensor(out=W3.d1[j][:], in0=W3.d1[j][:], in1=W3.rstd[:],
                                                                    op=ALU.mult),
                             reads=[W3.d1_b[j], W3.rstd_b], writes=[W3.d1_b[j]])
                        P.op("act", lambda e, k=k, j=j: e.activation(out=W3.sT[:, k, :], in_=W3.d1[j][:], func=AF.Silu,
                                                                     scale=W3.lg[:, k:k + 1], bias=W3.lb[:, k:k + 1]),
                             reads=[W3.d1_b[j], W3.lg_b, W3.lb_b], writes=[W3.sT_b[k]])
                    for dk in range(8):
                        j = dk % 2
                        mm_group(P, W3.yps[j][:], W3.yps_b[j],
                                 [(W3.wcp[:, k, dk * 128:(dk + 1) * 128], W3.sT[:, k, :]) for k in range(8)],
                                 [W3.wcp_b] + W3.sT_b)
                        P.op("dve", lambda e, dk=dk, j=j: e.tensor_tensor(out=W3.ma[j][:], in0=W3.yps[j][:],
                                                                          in1=W3.ga[:, dk, :], op=ALU.mult),
                             reads=[W3.yps_b[j], W3.ga_b], writes=[W3.ma_b[j]])
                        P.dma("sp", lambda e, dk=dk, j=j: e.dma_start(
                            out=maT_d[dk, :, ti * 512:(ti + 1) * 512], in_=W3.ma[j][:]),
                              reads=[W3.ma_b[j]], writes=[ma_b[ti]])

                conv_stage(0)
                for ti in range(NT):
                    if ti + 1 < NT:
                        conv_stage(ti + 1)
                    ln_stage(ti)
                P.barrier()

        if "attn" in phases:
            with contextlib.ExitStack() as s4:
                W4 = Ctx()
                NKMAX = NTOK
                W4.wap = sb(nc, s4, "p4wap", [128, NH, D], BF16)
                W4.wap_b = Buf()
                P.op("pool", lambda e: e.memset(W4.wap[64:128, :, :], 0.0), writes=[W4.wap_b])
                P.dma("pool", lambda e: e.dma_start(out=W4.wap[0:64, :, :], in_=w_attn_proj.rearrange("(h d) n -> d h n", d=HD)),
                      writes=[W4.wap_b])
                W4.wo = sb(nc, s4, "p4wo", [128, 8, D], BF16)
                W4.wo_b = Buf()
                load_w_bf16(P, W4.wo, W4.wo_b, w_out, 8)
                W4.cb = sb(nc, s4, "p4cb", [128, 4, 512], BF16)
                W4.cb_b = Buf()
                W4.kh = [sb(nc, s4, "p4kh%d" % j, [128, NKMAX], BF16) for j in range(2)]
                W4.kh_b = [Buf() for _ in range(2)]
                P.op("pool", lambda e: e.memset(W4.cb[:], 0.0), writes=[W4.cb_b])
                for jb in range(2):
                    for kc in range(2):
                        P.op("pool", lambda e, jb=jb, kc=kc: e.affine_select(
                            out=W4.cb[:, jb * 2 + kc, jb * 256:(jb + 1) * 256],
                            in_=W4.cb[:, jb * 2 + kc, jb * 256:(jb + 1) * 256], pattern=[[1, 256]],
                            compare_op=ALU.is_ge, fill=MASKNEG, base=-kc * 128, channel_multiplier=-1),
                             reads=[W4.cb_b], writes=[W4.cb_b])
                for j in range(2):
                    khv = W4.kh[j][64:96, :].rearrange("p (s k) -> p s k", k=256)
                    P.op("pool", lambda e, khv=khv: e.iota(khv, [[1, 2 * NB], [0, 256]], base=0, channel_multiplier=-1,
                                                           allow_small_or_imprecise_dtypes=True), writes=[W4.kh_b[j]])
                    P.op("dve", lambda e, khv=khv: e.tensor_single_scalar(out=khv, in_=khv, scalar=0.0, op=ALU.is_equal),
                         reads=[W4.kh_b[j]], writes=[W4.kh_b[j]])
                W4.vh = [sb(nc, s4, "p4vh%d" % j, [128, NCH, HD + 1], BF16) for j in range(2)]
                W4.vh_b = [Buf() for _ in range(2)]
                W4.qh = [sb(nc, s4, "p4qh%d" % j, [128, 512], BF16) for j in range(2)]
                W4.qh_b = [Buf() for _ in range(2)]
                for j in range(2):
                    P.op("pool", lambda e, j=j: e.memset(W4.qh[j][96:128, :], 0.0), writes=[W4.qh_b[j]])
                    P.op("pool", lambda e, j=j: e.memset(W4.kh[j][96:128, :], 0.0), writes=[W4.kh_b[j]])
                W4.pT = [sb(nc, s4, "p4pT%d" % j, [128, 512], BF16) for j in range(4)]
                W4.pT_b = [Buf() for _ in range(4)]
                W4.sps = [ps(nc, s4, "p4sps%d" % j, [128, 512], F32) for j in range(4)]
                W4.sps_b = [Buf() for _ in range(4)]
                W4.ops_ = [ps(nc, s4, "p4ops%d" % j, [128, 512], F32) for j in range(2)]
                W4.ops_b = [Buf() for _ in range(2)]
                W4.bps = ps(nc, s4, "p4bps", [128, 512], F32)
                W4.bps_b = Buf()
                _yps = ps(nc, s4, "p4yps0", [128, 512], F32)
                _ypsb = Buf()
                W4.yps = [_yps, _yps]
                W4.yps_b = [_ypsb, _ypsb]
                W4.rinv = sb(nc, s4, "p4rinv", [128, 512], F32)
                W4.rinv_b = Buf()
                W4.onesr = sb(nc, s4, "p4onesr", [128, 64], F32)
                W4.onesr_b = Buf()
                P.op("pool", lambda e: e.memset(W4.onesr[:], 1.0), writes=[W4.onesr_b])
                W4.bc = sb(nc, s4, "p4bc", [64, 512], F32)
                W4.bc_b = Buf()
                W4.oT = sb(nc, s4, "p4oT", [128, NH, 512], BF16)
                W4.oT_b = [Buf() for _ in range(NH)]
                P.op("pool", lambda e: e.memset(W4.oT[64:128, :, :], 0.0), writes=W4.oT_b)
                W4.ma = sb(nc, s4, "p4ma", [128, 8, 512], F32)
                W4.ma_b = Buf()
                W4.gb = sb(nc, s4, "p4gb", [128, 8, 512], F32)
                W4.gb_b = Buf()
                W4.x1 = sb(nc, s4, "p4x1", [128, 4, D], F32)
                W4.x1_b = [Buf() for _ in range(4)]
                W4.tmp = [sb(nc, s4, "p4tmp%d" % j, [128, 512], F32) for j in range(2)]
                W4.tmp_b = [Buf() for _ in range(2)]
                W4.mx = sb(nc, s4, "p4mx", [128, 8, 512], BF16)
                W4.mx_b = [Buf() for _ in range(8)]
                hi = 0
                si = 0
                pending = None
                for ti in range(NT):
                    nslots = 4 * ti + 4
                    nch = nslots * 2
                    nk = nch * 128
                    ownch = {8 * ti + 2: 0, 8 * ti + 3: 1, 8 * ti + 6: 2, 8 * ti + 7: 3}
                    P.dma("sp", lambda e, ti=ti: e.dma_start(
                        out=W4.ma[:], in_=maT_d[:, :, ti * 512:(ti + 1) * 512].rearrange("k p t -> p k t")),
                          reads=[ma_b[ti]], writes=[W4.ma_b])
                    P.dma("sp", lambda e, ti=ti: e.dma_start(
                        out=W4.gb[:], in_=gbT_d[:, :, ti * 512:(ti + 1) * 512].rearrange("k p t -> p k t")),
                          reads=[g_b[ti]], writes=[W4.gb_b])
                    for s in range(4):
                        P.dma("sp", lambda e, ti=ti, s=s: e.dma_start(
                            out=W4.x1[:, s, :], in_=x1_d[ti * 512 + s * 128:ti * 512 + (s + 1) * 128, :]),
                              reads=[x1_b[ti]], writes=[W4.x1_b[s]])
                    for h in range(NH):
                        j = hi % 2
                        hi += 1
                        kh, khb, vh, vhb, qh, qhb = W4.kh[j], W4.kh_b[j], W4.vh[j], W4.vh_b[j], W4.qh[j], W4.qh_b[j]
                        P.dma("sp", lambda e, kh=kh, h=h, nk=nk: e.dma_start(out=kh[0:64, 0:nk], in_=kT_d[h, :, 0:nk]),
                              reads=kv_b, writes=[khb])
                        P.dma("sp", lambda e, vh=vh, h=h, nch=nch: e.dma_start(out=vh[:, 0:nch, :], in_=v_d[h, :, 0:nch, :]),
                              reads=kv_b, writes=[vhb])
                        P.dma("sp", lambda e, qh=qh, h=h, ti=ti: e.dma_start(out=qh[0:96, :], in_=qaT_d[h, :, ti * 512:(ti + 1) * 512]),
                              reads=[qa_b[ti]], writes=[qhb])
                        op_, opb = W4.ops_[h % 2], W4.ops_b[h % 2]
                        LA = 3
                        sjs = {}
                        for kc in range(nch + LA):
                            if kc < nch:
                                sj = si % 4
                                si += 1
                                sjs[kc] = sj
                                sp_, spb = W4.sps[sj], W4.sps_b[sj]
                                pairs = [(kh[:, kc * 128:(kc + 1) * 128], qh[:])]
                                rd = [khb, qhb]
                                if kc in ownch:
                                    pairs.append((c.idb[:], W4.cb[:, ownch[kc], :]))
                                    rd = rd + [c.idb_b, W4.cb_b]
                                mm_group(P, sp_[:], spb, pairs, rd)
                                P.op("act", lambda e, sj=sj, sp_=sp_: e.activation(out=W4.pT[sj][:], in_=sp_[:], func=AF.Exp,
                                                                                   scale=HD ** -0.5),
                                     reads=[spb], writes=[W4.pT_b[sj]])
                            if kc == min(12, nch - 1) and pending is not None:
                                pending()
                                pending = None
                            if kc >= LA:
                                k2 = kc - LA
                                sj = sjs[k2]
                                P.op("pe", lambda e, k2=k2, sj=sj, vh=vh, op_=op_, nch=nch: e.matmul(
                                    out=op_[0:65, :], lhsT=vh[:, k2, :], rhs=W4.pT[sj][:], start=(k2 == 0), stop=(k2 == nch - 1)),
                                     reads=[vhb, W4.pT_b[sj]], writes=[opb], signal=(k2 == nch - 1))

                        def norm_head(h=h, op_=op_, opb=opb):
                            P.op("dve", lambda e: e.reciprocal(out=W4.rinv[64:65, :], in_=op_[64:65, :]),
                                 reads=[opb], writes=[W4.rinv_b])
                            P.op("pe", lambda e: e.matmul(out=W4.bps[0:64, :], lhsT=W4.onesr[64:65, :], rhs=W4.rinv[64:65, :],
                                                          start=True, stop=True),
                                 reads=[W4.onesr_b, W4.rinv_b], writes=[W4.bps_b])
                            P.op("act", lambda e: e.activation(out=W4.bc[:], in_=W4.bps[0:64, :], func=AF.Copy),
                                 reads=[W4.bps_b], writes=[W4.bc_b])
                            P.op("dve", lambda e: e.tensor_tensor(out=W4.oT[0:64, h, :], in0=op_[0:64, :], in1=W4.bc[:],
                                                                  op=ALU.mult),
                                 reads=[opb, W4.bc_b], writes=[W4.oT_b[h]])
                        pending = norm_head
                    pending()
                    pending = None
                    for dk in range(8):
                        j = dk % 2
                        mm_group(P, W4.yps[j][:], W4.yps_b[j],
                                 [(W4.wap[:, h, dk * 128:(dk + 1) * 128], W4.oT[:, h, :]) for h in range(NH)],
                                 [W4.wap_b] + W4.oT_b)
                        P.op("dve", lambda e, dk=dk, j=j: e.tensor_tensor(out=W4.tmp[j][:], in0=W4.yps[j][:],
                                                                          in1=W4.gb[:, dk, :], op=ALU.mult),
                             reads=[W4.yps_b[j], W4.gb_b], writes=[W4.tmp_b[j]])
                        P.op("pool", lambda e, dk=dk, j=j: e.tensor_tensor(out=W4.mx[:, dk, :], in0=W4.tmp[j][:],
                                                                           in1=W4.ma[:, dk, :], op=ALU.add),
                             reads=[W4.tmp_b[j], W4.ma_b], writes=[W4.mx_b[dk]])
                    q = 0
                    for s in range(4):
                        for hf in range(2):
                            j = q % 2
                            q += 1
                            mm_group(P, W4.yps[j][:], W4.yps_b[j],
                                     [(W4.mx[:, k, s * 128:(s + 1) * 128], W4.wo[:, k, hf * 512:(hf + 1) * 512])
                                      for k in range(8)], [W4.wo_b] + W4.mx_b)
                            P.op("dve", lambda e, s=s, hf=hf, j=j: e.tensor_tensor(
                                out=W4.x1[:, s, hf * 512:(hf + 1) * 512], in0=W4.yps[j][:],
                                in1=W4.x1[:, s, hf * 512:(hf + 1) * 512], op=ALU.add),
                                 reads=[W4.yps_b[j], W4.x1_b[s]], writes=[W4.x1_b[s]])
                        P.dma("sp", lambda e, ti=ti, s=s: e.dma_start(
                            out=x2_d[ti * 512 + s * 128:ti * 512 + (s + 1) * 128, :], in_=W4.x1[:, s, :]),
                              reads=[W4.x1_b[s]], writes=[x2_b[ti]])
                P.barrier()

        if "ffn2" in phases:
            tiles = [dict(src=x2_d[t * 512:(t + 1) * 512, :], own=True, idx=t, buf=x2_b[t]) for t in range(NT)]

            def epi2(W, tile, s, x, xb):
                i = tile["idx"]
                P.dma("sp", lambda e: e.dma_start(out=out[i * 512 + s * 128:i * 512 + (s + 1) * 128, :], in_=x[:]),
                      reads=[xb], writes=[])

            ffn_phase(P, nc, c, "f2", tiles, ffn2_norm, ffn2_wg, ffn2_wu, ffn2_wd, epi2)

        P.finish()
        P.emit(block)
    return nc


WNAMES = ["ffn1_norm", "ffn1_w_gate", "ffn1_w_up", "ffn1_w_down", "mix_norm", "w_in", "conv_dw", "conv_dw_bias",
          "conv_ln_gain", "conv_ln_bias", "w_conv_proj", "q_norm", "k_norm", "w_attn_proj", "w_out", "ffn2_norm",
          "ffn2_w_gate", "ffn2_w_up", "ffn2_w_down"]


def make_in_maps(inputs, s_half):
    x = np.ascontiguousarray(inputs["x"], dtype=np.float32)
    B = x.shape[0]
    nb = s_half // BLK
    shared = {}
    for n in WNAMES:
        a = np.asarray(inputs[n], dtype=np.float32)
        a = a[0]
        if a.ndim == 1:
            a = a.reshape(1, -1)
        shared[n] = np.ascontiguousarray(a)
    maps = []
    for cid in range(2 * B):
        b, par = cid // 2, cid % 2
        xb = x[b].reshape(2 * nb, BLK, D)
        m = dict(shared)
        m["x_own"] = np.ascontiguousarray(xb[par::2].reshape(s_half, D))
        prev = np.zeros((nb, BLK, D), np.float32)
        for j in range(nb):
            g = 2 * j + par - 1
            if g >= 0:
                prev[j] = xb[g]
        m["x_prev"] = prev.reshape(s_half, D)
        sbias = np.zeros((1, 32), np.float32)
        if par == 0:
            sbias[0, 0] = -1e30
        m["slot_bias"] = sbias
        maps.append(m)
    return maps


def gather_out(results, x_shape, s_half):
    B = x_shape[0]
    nb = s_half // BLK
    outp = np.empty(x_shape, np.float32)
    for cid in range(2 * B):
        b, par = cid // 2, cid % 2
        ob = outp[b].reshape(2 * nb, BLK, D)
        ob[par::2] = np.asarray(results[cid]["out"]).reshape(nb, BLK, D)
    return outp


def kernel(**inputs):
    nc = bass.Bass("TRN2", target_bir_lowering=False)
    build(nc, S_HALF)
    maps = make_in_maps(inputs, S_HALF)
    res = run_bass_kernel_spmd(nc, maps, core_ids=list(range(NCORES)))
    return gather_out(res.results, inputs["x"].shape, S_HALF)
```

```python
import contextlib
import numpy as np
import concourse.bass as bass
import concourse.mybir as mybir
from concourse.bass_utils import run_bass_kernel_spmd

F32 = mybir.dt.float32
BF16 = mybir.dt.bfloat16
ALU = mybir.AluOpType
AF = mybir.ActivationFunctionType

D = 1024
DFF = 2816
NH = 16
HD = 64
CW = 31
BLK = 256
TOPK = 3
EPS = 1e-6
INC = 7168
NCORES = 8
S_HALF = 4096
MASKNEG = -240000.0


class Buf:
    __slots__ = ("w", "r")

    def __init__(self):
        self.w = None
        self.r = {}


class Prog:
    COMPUTE = ("pe", "act", "dve", "pool")

    def __init__(self, nc, stack, n_dma=12):
        self.nc = nc
        self.ops = {e: [] for e in ("pe", "act", "dve", "pool", "sp")}
        self.sig = {e: 0 for e in self.COMPUTE}
        self.seen = {e: {} for e in self.ops}
        self.sems = {}
        for e in self.COMPUTE:
            self.sems[e] = stack.enter_context(nc.semaphore("s_" + e))
        self.dq = {}
        for q in ("sp", "pool"):
            keys = []
            for i in range(n_dma):
                k = "%s_d%d" % (q, i)
                self.sems[k] = stack.enter_context(nc.semaphore(k))
                keys.append(k)
            self.dq[q] = {"keys": keys, "cnt": {k: 0 for k in keys}, "i": 0}

    def _need(self, eng, deps):
        waits = []
        seen = self.seen[eng]
        for (k, v) in deps:
            if eng == "pe" and k == "pe":
                continue
            if seen.get(k, 0) >= v:
                continue
            seen[k] = v
            waits.append((k, v))
        return waits

    def _deps(self, reads, writes):
        deps = set()
        for b in reads:
            if b.w is not None:
                deps.add(b.w)
        for b in writes:
            if b.w is not None:
                deps.add(b.w)
            for k, v in b.r.items():
                deps.add((k, v))
        mx = {}
        for k, v in deps:
            if mx.get(k, 0) < v:
                mx[k] = v
        return list(mx.items())

    def _mark(self, ev, reads, writes):
        k, v = ev
        for b in reads:
            if b.r.get(k, 0) < v:
                b.r[k] = v
        for b in writes:
            b.w = ev
            b.r = {}

    def op(self, eng, fn, reads=(), writes=(), signal=True):
        waits = self._need(eng, self._deps(reads, writes))
        if signal:
            self.sig[eng] += 1
            ev = (eng, self.sig[eng])
        else:
            ev = (eng, self.sig[eng] + 1)
        self.ops[eng].append((waits, fn, (eng, 1) if signal else None))
        self._mark(ev, reads, writes)
        return ev

    def dma(self, q, fn, reads=(), writes=()):
        dq = self.dq[q]
        k = dq["keys"][dq["i"] % len(dq["keys"])]
        dq["i"] += 1
        deps = self._deps(reads, writes)
        if dq["cnt"][k] > 0:
            deps.append((k, dq["cnt"][k]))
        waits = self._need(q, deps)
        dq["cnt"][k] += 16
        ev = (k, dq["cnt"][k])
        self.ops[q].append((waits, fn, (k, 16)))
        self._mark(ev, reads, writes)
        return ev

    def all_events(self):
        evs = [(e, self.sig[e]) for e in self.COMPUTE if self.sig[e] > 0]
        for q in self.dq.values():
            for k, c in q["cnt"].items():
                if c > 0:
                    evs.append((k, c))
        return evs

    def barrier(self):
        evs = self.all_events()
        for eng in self.ops:
            waits = self._need(eng, [e for e in evs if not (e[0] == eng)])
            if waits:
                self.ops[eng].append((waits, None, None))

    def finish(self):
        evs = self.all_events()
        waits = self._need("sp", evs)
        self.ops["sp"].append((waits, None, None))

    def emit(self, block):
        sems = self.sems

        def run(eng_obj, lst):
            for waits, fn, inc in lst:
                for k, v in waits:
                    eng_obj.wait_ge(sems[k], v)
                if fn is not None:
                    ins = fn(eng_obj)
                    if inc is not None:
                        ins.then_inc(sems[inc[0]], inc[1])

        ops = self.ops

        @block.tensor
        def _(e):
            run(e, ops["pe"])

        @block.scalar
        def _(e):
            run(e, ops["act"])

        @block.vector
        def _(e):
            run(e, ops["dve"])

        @block.gpsimd
        def _(e):
            run(e, ops["pool"])

        @block.sync
        def _(e):
            run(e, ops["sp"])


class Ctx:
    pass


def sb(nc, st, name, shape, dt):
    return st.enter_context(nc.sbuf_tensor(name, shape, dt))


def ps(nc, st, name, shape, dt):
    return st.enter_context(nc.psum_tensor(name, shape, dt))


def load_w_bf16(P, dst, dst_buf, src, nchunk):
    step = 8 if nchunk <= 8 else 11
    for k0 in range(0, nchunk, step):
        k1 = min(nchunk, k0 + step)
        P.dma("pool", lambda e, k0=k0, k1=k1: e.dma_start(
            out=dst[:, k0:k1, :], in_=src[k0 * 128:k1 * 128, :].rearrange("(k p) n -> p k n", p=128)),
              writes=[dst_buf])


def make_identity(P, nc, st, c):
    c.idf = sb(nc, st, "idf", [128, 128], F32)
    c.idb = sb(nc, st, "idb", [128, 128], BF16)
    c.idf_b = Buf()
    c.idb_b = Buf()
    tmp = sb(nc, st, "idtmp", [128, 128], F32)
    tb = Buf()
    P.op("pool", lambda e: e.iota(tmp[:], [[1, 128]], base=0, channel_multiplier=-1,
                                  allow_small_or_imprecise_dtypes=True), writes=[tb])
    P.op("pool", lambda e: e.tensor_single_scalar(out=c.idf[:], in_=tmp[:], scalar=0.0, op=ALU.is_equal),
         reads=[tb], writes=[c.idf_b])
    P.op("pool", lambda e: e.tensor_copy(out=c.idb[:], in_=c.idf[:]), reads=[c.idf_b], writes=[c.idb_b])


def norm_pre(P, c, W, xsrc, xsrc_buf):
    i = W.nt_i
    W.nt_i += 1
    j = i % len(W.xn)
    xn, xnb = W.xn[j], W.xn_b[j]
    ssc = W.ss[:, i % 8:i % 8 + 1]
    ssb = W.ss_b[i % 8]
    P.op("act", lambda e: e.activation(out=xn[:], in_=xsrc, func=AF.Square, accum_out=ssc),
         reads=[xsrc_buf], writes=[xnb, ssb])
    msc = W.ms[:, i % 8:i % 8 + 1]
    msb = W.ms_b[i % 8]
    P.op("dve", lambda e: e.tensor_scalar(out=msc, in0=ssc, scalar1=1.0 / D, scalar2=EPS,
                                         op0=ALU.mult, op1=ALU.add), reads=[ssb], writes=[msb])
    rsc = W.rs[:, i % 8:i % 8 + 1]
    rsb = W.rs_b[i % 8]
    P.op("pool", lambda e: e.tensor_tensor(out=rsc, in0=msc, in1=c.mhalf[:, 0:1], op=ALU.pow),
         reads=[msb, c.mhalf_b], writes=[rsb])
    P.op("dve", lambda e: e.tensor_scalar(out=xn[:], in0=xsrc, scalar1=rsc, scalar2=None, op0=ALU.mult),
         reads=[xsrc_buf, rsb], writes=[xnb])
    return xn, xnb


def norm_T(P, c, W, xn, xnb, gcol, gcol_buf, dstT, dstT_buf, s):
    j = W.tp_i % 2
    W.tp_i += 1
    tp, tpb = W.tp[j], W.tp_b[j]
    for k in range(8):
        P.op("pe", lambda e, k=k: e.transpose(out=tp[:, k * 128:(k + 1) * 128], in_=xn[:, k * 128:(k + 1) * 128],
                                              identity=c.idb[:]),
             reads=[xnb, c.idb_b], writes=[tpb], signal=(k == 7))
    P.op("dve", lambda e: e.tensor_tensor(out=dstT[:, :, s * 128:(s + 1) * 128],
                                         in0=tp[:].rearrange("p (k t) -> p k t", t=128),
                                         in1=gcol[:, 0:8].unsqueeze(2).to_broadcast([128, 8, 128]), op=ALU.mult),
         reads=[tpb, gcol_buf], writes=[dstT_buf])


def alloc_norm_ws(nc, st, W, pfx, nxn=4):
    W.nt_i = 0
    W.tp_i = 0
    W.ss = sb(nc, st, pfx + "ss", [128, 8], F32)
    W.ms = sb(nc, st, pfx + "ms", [128, 8], F32)
    W.rs = sb(nc, st, pfx + "rs", [128, 8], F32)
    W.ss_b = [Buf() for _ in range(8)]
    W.ms_b = [Buf() for _ in range(8)]
    W.rs_b = [Buf() for _ in range(8)]
    W.xn = [sb(nc, st, pfx + "xn%d" % j, [128, D], BF16) for j in range(nxn)]
    W.xn_b = [Buf() for _ in range(nxn)]
    W.tp = [ps(nc, st, pfx + "tp%d" % j, [128, D], BF16) for j in range(2)]
    W.tp_b = [Buf() for _ in range(2)]


def load_col(P, nc, st, name, src_row, n):
    t = sb(nc, st, name, [128, n], F32)
    b = Buf()
    P.dma("sp", lambda e: e.dma_start(out=t[:], in_=src_row.rearrange("o (k p) -> p (o k)", p=128),
                                      allow_slow_non_contiguous=True), writes=[b])
    return t, b


def ffn_phase(P, nc, c, pfx, tiles, g_row, wg, wu, wd, epilogue, g2_row=None, tile_end=None):
    with contextlib.ExitStack() as st:
        W = Ctx()
        W.wg = sb(nc, st, pfx + "wg", [128, 8, DFF], BF16)
        W.wu = sb(nc, st, pfx + "wu", [128, 8, DFF], BF16)
        W.wd = sb(nc, st, pfx + "wd", [128, 22, D], BF16)
        W.wd_b = Buf()
        W.wg_b = [Buf(), Buf()]
        W.wu_b = [Buf(), Buf()]
        for half in range(2):
            c0, c1 = half * 1408, (half + 1) * 1408
            for (dst, src, bufs) in ((W.wg, wg, W.wg_b), (W.wu, wu, W.wu_b)):
                P.dma("pool", lambda e, dst=dst, src=src, c0=c0, c1=c1: e.dma_start(
                    out=dst[:, :, c0:c1], in_=src[:, c0:c1].rearrange("(k p) n -> p k n", p=128)), writes=[bufs[half]])
        load_w_bf16(P, W.wd, W.wd_b, wd, 22)

        def fblk(f):
            return 0 if f < 11 else 1
        W.gcol, W.gcol_b = load_col(P, nc, st, pfx + "gcol", g_row, 8)
        if g2_row is not None:
            W.g2col, W.g2col_b = load_col(P, nc, st, pfx + "g2col", g2_row, 8)
        alloc_norm_ws(nc, st, W, pfx, nxn=4)
        NXS = 3
        W.xs = [sb(nc, st, pfx + "xs%d" % j, [128, D], F32) for j in range(NXS)]
        W.xs_b = [Buf() for _ in range(NXS)]
        W.xs_i = 0
        W.hT = [sb(nc, st, pfx + "hT%d" % j, [128, 8, 512], BF16) for j in range(2)]
        W.hT_b = [Buf() for _ in range(2)]
        W.actT = sb(nc, st, pfx + "actT", [128, 22, 512], BF16)
        W.actT_b = [Buf() for _ in range(22)]
        W.sg = [sb(nc, st, pfx + "sg%d" % j, [128, 512], BF16) for j in range(2)]
        W.sg_b = [Buf() for _ in range(2)]
        W.gps = [ps(nc, st, pfx + "gps%d" % j, [128, 512], F32) for j in range(2)]
        W.ups = [ps(nc, st, pfx + "ups%d" % j, [128, 512], F32) for j in range(2)]
        W.gps_b = [Buf() for _ in range(2)]
        W.ups_b = [Buf() for _ in range(2)]
        W.dps = [ps(nc, st, pfx + "dps%d" % j, [128, 512], F32) for j in range(2)]
        W.dps_b = [Buf() for _ in range(2)]
        if g2_row is not None:
            W.h2T = sb(nc, st, pfx + "h2T", [128, 8, 512], BF16)
            W.h2T_b = Buf()

        def load_x(tile, s):
            j = W.xs_i % NXS
            W.xs_i += 1
            src = tile["src"]
            P.dma("sp", lambda e: e.dma_start(out=W.xs[j][:], in_=src[s * 128:(s + 1) * 128, :]),
                  reads=[tile["buf"]] if tile.get("buf") else [], writes=[W.xs_b[j]])
            return W.xs[j], W.xs_b[j]

        def n_pre(tile):
            res = []
            for s in range(4):
                x, xb = load_x(tile, s)
                res.append(norm_pre(P, c, W, x[:], xb))
            return res

        def n_T(t, xns):
            for s in range(4):
                norm_T(P, c, W, xns[s][0], xns[s][1], W.gcol, W.gcol_b, W.hT[t % 2], W.hT_b[t % 2], s)

        xns = n_pre(tiles[0])
        n_T(0, xns)
        for t, tile in enumerate(tiles):
            hT, hTb = W.hT[t % 2], W.hT_b[t % 2]
            nxt = None
            if t + 1 < len(tiles):
                nxt = n_pre(tiles[t + 1])
            for f in range(22):
                j = f % 2
                for k in range(8):
                    P.op("pe", lambda e, f=f, k=k, j=j, hT=hT: e.matmul(out=W.gps[j][:], lhsT=W.wg[:, k, f * 128:(f + 1) * 128],
                                                                 rhs=hT[:, k, :], start=(k == 0), stop=(k == 7)),
                         reads=[W.wg_b[fblk(f)], hTb], writes=[W.gps_b[j]], signal=(k == 7))
                for k in range(8):
                    P.op("pe", lambda e, f=f, k=k, j=j, hT=hT: e.matmul(out=W.ups[j][:], lhsT=W.wu[:, k, f * 128:(f + 1) * 128],
                                                                 rhs=hT[:, k, :], start=(k == 0), stop=(k == 7)),
                         reads=[W.wu_b[fblk(f)], hTb], writes=[W.ups_b[j]], signal=(k == 7))
                P.op("act", lambda e, j=j: e.activation(out=W.sg[j][:], in_=W.gps[j][:], func=AF.Silu),
                     reads=[W.gps_b[j]], writes=[W.sg_b[j]])
                P.op("dve", lambda e, j=j, f=f: e.tensor_tensor(out=W.actT[:, f, :], in0=W.ups[j][:], in1=W.sg[j][:],
                                                                op=ALU.mult),
                     reads=[W.ups_b[j], W.sg_b[j]], writes=[W.actT_b[f]])
            if nxt is not None:
                n_T(t + 1, nxt)
            q = 0
            pend = None
            for s in range(4):
                x, xb = load_x(tile, s)
                for hf in range(2):
                    j = q % 2
                    q += 1
                    for f in range(22):
                        P.op("pe", lambda e, f=f, s=s, hf=hf, j=j: e.matmul(
                            out=W.dps[j][:], lhsT=W.actT[:, f, s * 128:(s + 1) * 128],
                            rhs=W.wd[:, f, hf * 512:(hf + 1) * 512], start=(f == 0), stop=(f == 21)),
                             reads=[W.actT_b[f], W.wd_b], writes=[W.dps_b[j]], signal=(f == 21))
                    P.op("dve", lambda e, x=x, hf=hf, j=j: e.scalar_tensor_tensor(
                        out=x[:, hf * 512:(hf + 1) * 512], in0=W.dps[j][:], scalar=0.5,
                        in1=x[:, hf * 512:(hf + 1) * 512], op0=ALU.mult, op1=ALU.add),
                         reads=[W.dps_b[j], xb], writes=[xb])
                if pend is not None:
                    pend()
                pend = (lambda tile=tile, s=s, x=x, xb=xb: epilogue(W, tile, s, x, xb))
            pend()
            pend = None
            if tile_end is not None:
                tile_end(W, tile)
        P.barrier()


def mm_group(P, out, out_b, pairs, reads):
    n = len(pairs)
    for i, (l, r) in enumerate(pairs):
        P.op("pe", lambda e, l=l, r=r, i=i: e.matmul(out=out, lhsT=l, rhs=r, start=(i == 0), stop=(i == n - 1)),
             reads=reads, writes=[out_b], signal=(i == n - 1))


def head_norm(P, c, W, src_ps, src_b, gbc, gbc_b, dst32, dst32_b, hf):
    i = W.hn_i
    W.hn_i += 1
    r = i % len(W.raw)
    raw, rawb = W.raw[r], W.raw_b[r]
    P.op("act", lambda e: e.activation(out=raw[:], in_=src_ps, func=AF.Copy), reads=[src_b], writes=[rawb])
    j = i % len(W.sq)
    sq, sqb = W.sq[j], W.sq_b[j]
    P.op("act", lambda e: e.activation(out=sq[:], in_=raw[:], func=AF.Square), reads=[rawb], writes=[sqb])
    hs, hsb = W.hs[j], W.hs_b[j]
    P.op("dve", lambda e: e.tensor_reduce(out=hs[:, 0:8], in_=sq[:].rearrange("p (h d) -> p h d", d=HD),
                                         axis=mybir.AxisListType.X, op=ALU.add), reads=[sqb], writes=[hsb])
    P.op("dve", lambda e: e.tensor_scalar(out=hs[:, 8:16], in0=hs[:, 0:8], scalar1=1.0 / HD, scalar2=EPS,
                                         op0=ALU.mult, op1=ALU.add), reads=[hsb], writes=[hsb])
    P.op("pool", lambda e: e.tensor_tensor(out=hs[:, 16:24], in0=hs[:, 8:16], in1=c.mhalf[:, 0:8], op=ALU.pow),
         reads=[hsb, c.mhalf_b], writes=[hsb])
    P.op("dve", lambda e: e.tensor_tensor(out=sq[:].rearrange("p (h d) -> p h d", d=HD),
                                         in0=raw[:].rearrange("p (h d) -> p h d", d=HD),
                                         in1=hs[:, 16:24].unsqueeze(2).to_broadcast([128, 8, HD]), op=ALU.mult),
         reads=[rawb, hsb], writes=[sqb])
    P.op("pool", lambda e: e.tensor_tensor(out=dst32[:, hf * 512:(hf + 1) * 512].rearrange("p (h d) -> p h d", d=HD),
                                          in0=sq[:].rearrange("p (h d) -> p h d", d=HD),
                                          in1=gbc[:].unsqueeze(1).to_broadcast([128, 8, HD]), op=ALU.mult),
         reads=[sqb, gbc_b], writes=[dst32_b])


def build(nc, s_half=S_HALF, phases=("ffn1", "proj", "conv", "attn", "ffn2"), debug=()):
    NT = s_half // 512
    NB = s_half // BLK
    NTOK = 2 * s_half
    NCH = NTOK // 128
    ins = {}

    def din(name, shape):
        ins[name] = nc.dram_tensor(name, shape, F32, kind="ExternalInput").ap()
        return ins[name]

    x_own = din("x_own", [s_half, D])
    x_prev = din("x_prev", [s_half, D])
    slot_bias = din("slot_bias", [1, 32])
    ffn1_norm = din("ffn1_norm", [1, D])
    ffn1_wg = din("ffn1_w_gate", [D, DFF])
    ffn1_wu = din("ffn1_w_up", [D, DFF])
    ffn1_wd = din("ffn1_w_down", [DFF, D])
    mix_norm = din("mix_norm", [1, D])
    w_in = din("w_in", [D, INC])
    conv_dw = din("conv_dw", [CW, D])
    conv_dw_bias = din("conv_dw_bias", [1, D])
    conv_ln_gain = din("conv_ln_gain", [1, D])
    conv_ln_bias = din("conv_ln_bias", [1, D])
    w_conv_proj = din("w_conv_proj", [D, D])
    q_norm = din("q_norm", [1, HD])
    k_norm = din("k_norm", [1, HD])
    w_attn_proj = din("w_attn_proj", [D, D])
    w_out = din("w_out", [D, D])
    ffn2_norm = din("ffn2_norm", [1, D])
    ffn2_wg = din("ffn2_w_gate", [D, DFF])
    ffn2_wu = din("ffn2_w_up", [D, DFF])
    ffn2_wd = din("ffn2_w_down", [DFF, D])
    out = nc.dram_tensor("out", [s_half, D], F32, kind="ExternalOutput").ap()

    def scratch(name, shape, dt):
        if name in debug:
            return nc.dram_tensor("dbg_" + name, shape, dt, kind="ExternalOutput").ap()
        return nc.dram_tensor(name, shape, dt, kind="Internal").ap()

    x1_d = scratch("x1_d", [s_half, D], F32)
    x2_d = scratch("x2_d", [s_half, D], F32)
    h2T_d = scratch("h2T_d", [8, 128, NTOK], BF16)
    kT_d = scratch("kT_d", [NH, HD, NTOK], BF16)
    v_d = scratch("v_d", [NH, 128, NCH, HD + 1], BF16)
    qaT_d = scratch("qaT_d", [NH, 96, s_half], BF16)
    uT_d = scratch("uT_d", [8, 128, NB * 288], BF16)
    gaT_d = scratch("gaT_d", [8, 128, s_half], F32)
    gbT_d = scratch("gbT_d", [8, 128, s_half], F32)
    maT_d = scratch("maT_d", [8, 128, s_half], F32)
    x1_b = [Buf() for _ in range(NT)]
    x2_b = [Buf() for _ in range(NT)]
    h2T_b = [Buf() for _ in range(2 * NT)]
    kv_b = [Buf() for _ in range(2 * NT)]
    qa_b = [Buf() for _ in range(NT)]
    u_b = [Buf() for _ in range(2 * NT)]
    g_b = [Buf() for _ in range(NT)]
    ma_b = [Buf() for _ in range(NT)]

    with contextlib.ExitStack() as st:
        P = Prog(nc, st)
        block = st.enter_context(nc.Block())
        c = Ctx()
        make_identity(P, nc, st, c)
        c.mhalf = sb(nc, st, "mhalf", [128, 16], F32)
        c.mhalf_b = Buf()
        P.op("pool", lambda e: e.memset(c.mhalf[:], -0.5), writes=[c.mhalf_b])
        c.kmbd = sb(nc, st, "kmbd", [128, 8, 64], F32)
        c.kmbd_b = Buf()
        P.op("pool", lambda e: e.memset(c.kmbd[:], 0.0), writes=[c.kmbd_b])

        if "ffn1" in phases:
            tiles = []
            for t in range(NT):
                tiles.append(dict(src=x_prev[t * 512:(t + 1) * 512, :], own=False, idx=t))
            for t in range(NT):
                tiles.append(dict(src=x_own[t * 512:(t + 1) * 512, :], own=True, idx=t))

            def epi1(W, tile, s, x, xb):
                if tile["own"]:
                    i = tile["idx"]
                    P.dma("sp", lambda e: e.dma_start(out=x1_d[i * 512 + s * 128:i * 512 + (s + 1) * 128, :], in_=x[:]),
                          reads=[xb], writes=[x1_b[i]])
                xn, xnb = norm_pre(P, c, W, x[:], xb)
                norm_T(P, c, W, xn, xnb, W.g2col, W.g2col_b, W.h2T, W.h2T_b, s)

            def end1(W, tile):
                gi = tile["idx"] + (NT if tile["own"] else 0)
                P.dma("sp", lambda e: e.dma_start(
                    out=h2T_d[:, :, gi * 512:(gi + 1) * 512].rearrange("k p t -> p k t"), in_=W.h2T[:]),
                      reads=[W.h2T_b], writes=[h2T_b[gi]])

            ffn_phase(P, nc, c, "f1", tiles, ffn1_norm, ffn1_wg, ffn1_wu, ffn1_wd, epi1, g2_row=mix_norm, tile_end=end1)

        if "proj" in phases:
            with contextlib.ExitStack() as s2:
                W2 = Ctx()
                W2.w = sb(nc, s2, "p2w", [128, 8, INC], BF16)
                _wkv, _wq, _wcv, _wgt = Buf(), Buf(), Buf(), Buf()
                W2.w_b = {"kv": _wkv, "q": _wq, "conv": _wcv, "gate": _wgt}
                for (c0, c1, bl) in ((3072, 5120, [_wkv]), (0, 3072, [_wq, _wcv]), (5120, 7168, [_wgt])):
                    P.dma("pool", lambda e, c0=c0, c1=c1: e.dma_start(
                        out=W2.w[:, :, c0:c1], in_=w_in[:, c0:c1].rearrange("(k p) n -> p k n", p=128)), writes=bl)

                def wbuf(col0):
                    return W2.w_b["conv" if col0 < 2048 else "q" if col0 < 3072 else "kv" if col0 < 5120 else "gate"]
                W2.gq = sb(nc, s2, "p2gq", [128, HD], F32)
                W2.gk = sb(nc, s2, "p2gk", [128, HD], F32)
                W2.gq_b, W2.gk_b = Buf(), Buf()
                P.dma("sp", lambda e: e.dma_start(out=W2.gq[:], in_=q_norm[0:1, :].to_broadcast([128, HD])), writes=[W2.gq_b])
                P.dma("sp", lambda e: e.dma_start(out=W2.gk[:], in_=k_norm[0:1, :].to_broadcast([128, HD])), writes=[W2.gk_b])
                W2.sbias = sb(nc, s2, "p2sbias", [128, 32], F32)
                W2.sbias_b = Buf()
                P.dma("sp", lambda e: e.dma_start(out=W2.sbias[:], in_=slot_bias[0:1, :].to_broadcast([128, 32])),
                      writes=[W2.sbias_b])
                W2.sbj = sb(nc, s2, "p2sbj", [128, 32], F32)
                W2.sbj_b = Buf()
                W2.ones32 = sb(nc, s2, "p2ones", [128, 1], F32)
                W2.ones32_b = Buf()
                P.op("pool", lambda e: e.memset(W2.ones32[:], 1.0), writes=[W2.ones32_b])
                W2.hT = [sb(nc, s2, "p2hT%d" % j, [128, 8, 512], BF16) for j in range(2)]
                W2.hT_b = [Buf() for _ in range(2)]
                W2.raw = [sb(nc, s2, "p2raw%d" % j, [128, 512], F32) for j in range(4)]
                W2.raw_b = [Buf() for _ in range(4)]
                W2.mm = [ps(nc, s2, "p2mm%d" % j, [128, 512], F32) for j in range(3)]
                W2.mm_b = [Buf() for _ in range(3)]
                W2.mm_i = 0
                W2.tp = ps(nc, s2, "p2tp", [128, 1024], BF16)
                W2.tp_b = Buf()
                W2.tq = ps(nc, s2, "p2tq", [128, 1024], F32)
                W2.tq_b = Buf()
                W2.gps = ps(nc, s2, "p2gps", [128, 512], F32)
                W2.gps_b = Buf()
                W2.kmps = ps(nc, s2, "p2kmps", [128, 16], F32)
                W2.kmps_b = Buf()
                W2.hn_i = 0
                W2.sq = [sb(nc, s2, "p2sq%d" % j, [128, 512], F32) for j in range(4)]
                W2.sq_b = [Buf() for _ in range(4)]
                W2.hs = [sb(nc, s2, "p2hs%d" % j, [128, 24], F32) for j in range(4)]
                W2.hs_b = [Buf() for _ in range(4)]
                W2.kn32 = [sb(nc, s2, "p2kn%d" % j, [128, D], F32) for j in range(2)]
                W2.kn32_b = [Buf() for _ in range(2)]
                W2.knb = sb(nc, s2, "p2knb", [128, D], BF16)
                W2.knb_b = Buf()
                W2.kT = sb(nc, s2, "p2kT", [128, 8, 128], BF16)
                W2.kT_b = Buf()
                W2.va = [sb(nc, s2, "p2va%d" % j, [128, NH, HD + 1], BF16) for j in range(2)]
                W2.va_b = [Buf() for _ in range(2)]
                for j in range(2):
                    P.op("pool", lambda e, j=j: e.memset(W2.va[j][:], 1.0), writes=[W2.va_b[j]])
                W2.qn32 = [sb(nc, s2, "p2qn%d" % j, [128, D], F32) for j in range(2)]
                W2.qn32_b = [Buf() for _ in range(2)]
                W2.qT32 = sb(nc, s2, "p2qT32", [128, 8, 128], F32)
                W2.qT32_b = Buf()
                W2.gate = sb(nc, s2, "p2gate", [128, NH, 32], F32)
                W2.gate_b = Buf()
                W2.top8 = sb(nc, s2, "p2top8", [128, NH, 8], F32)
                W2.top8_b = Buf()
                W2.sel = sb(nc, s2, "p2sel", [128, NH, 32], F32)
                W2.sel_b = Buf()
                W2.val = sb(nc, s2, "p2val", [128, NH, 32], F32)
                W2.val_b = Buf()
                W2.qa = sb(nc, s2, "p2qa", [128, NH, 96], BF16)
                W2.qa_b = Buf()
                W2.qaT = sb(nc, s2, "p2qaT", [96, NH, 128], BF16)
                W2.qaT_b = Buf()
                W2.ev = [sb(nc, s2, "p2ev%d" % j, [128, 512], F32) for j in range(2)]
                W2.ev_b = [Buf() for _ in range(2)]
                W2.uo = [sb(nc, s2, "p2uo%d" % j, [128, 512], BF16) for j in range(2)]
                W2.uo_b = [Buf() for _ in range(2)]
                W2.go = [sb(nc, s2, "p2go%d" % j, [128, 512], F32) for j in range(2)]
                W2.go_b = [Buf() for _ in range(2)]

                def nextmm():
                    j = W2.mm_i % 3
                    W2.mm_i += 1
                    return W2.mm[j], W2.mm_b[j]

                def tok_proj(hT, hTb, s, col0):
                    pt, pb = nextmm()
                    mm_group(P, pt[:], pb, [(hT[:, k, s * 128:(s + 1) * 128], W2.w[:, k, col0:col0 + 512])
                                            for k in range(8)], [hTb, wbuf(col0)])
                    return pt, pb

                def feat_proj(hT, hTb, col0):
                    pt, pb = nextmm()
                    mm_group(P, pt[:], pb, [(W2.w[:, k, col0:col0 + 128], hT[:, k, :]) for k in range(8)],
                             [hTb, wbuf(col0)])
                    return pt, pb

                def load_hT(gi):
                    hT, hTb = W2.hT[gi % 2], W2.hT_b[gi % 2]
                    P.dma("sp", lambda e: e.dma_start(
                        out=hT[:], in_=h2T_d[:, :, gi * 512:(gi + 1) * 512].rearrange("k p t -> p k t")),
                          reads=[h2T_b[gi]], writes=[hTb])

                def chunk_of(gi, s):
                    own = gi >= NT
                    j = 2 * (gi - NT if own else gi) + s // 2
                    return 4 * j + (2 if own else 0) + s % 2

                def stageA(gi, s):
                    own = gi >= NT
                    hT, hTb = W2.hT[gi % 2], W2.hT_b[gi % 2]
                    ch = chunk_of(gi, s)
                    par = ch % 2
                    kn, knb_ = W2.kn32[par], W2.kn32_b[par]
                    for hf in range(2):
                        pt, pb = tok_proj(hT, hTb, s, 3072 + hf * 512)
                        head_norm(P, c, W2, pt[:], pb, W2.gk, W2.gk_b, kn, knb_, hf)
                    va, vab = W2.va[par], W2.va_b[par]
                    for hf in range(2):
                        pt, pb = tok_proj(hT, hTb, s, 4096 + hf * 512)
                        P.op("act", lambda e, pt=pt, hf=hf: e.activation(
                            out=va[:, hf * 8:(hf + 1) * 8, 0:HD], in_=pt[:].rearrange("p (h d) -> p h d", d=HD),
                            func=AF.Copy), reads=[pb], writes=[vab])
                    P.dma("sp", lambda e: e.dma_start(out=v_d[:, :, ch, :].rearrange("h p d -> p h d"), in_=va[:]),
                          reads=[vab], writes=[kv_b[gi]])
                    if own:
                        qn, qnb = W2.qn32[par], W2.qn32_b[par]
                        for hf in range(2):
                            pt, pb = tok_proj(hT, hTb, s, 2048 + hf * 512)
                            head_norm(P, c, W2, pt[:], pb, W2.gq, W2.gq_b, qn, qnb, hf)

                def stageB(gi, s):
                    own = gi >= NT
                    ti = gi - NT
                    ch = chunk_of(gi, s)
                    slot = ch // 2
                    par = ch % 2
                    kn, knb_ = W2.kn32[par], W2.kn32_b[par]
                    P.op("act", lambda e: e.activation(out=W2.knb[:], in_=kn[:], func=AF.Copy),
                         reads=[knb_], writes=[W2.knb_b])
                    for k in range(8):
                        P.op("pe", lambda e, k=k: e.transpose(out=W2.tp[:, k * 128:(k + 1) * 128],
                                                              in_=W2.knb[:, k * 128:(k + 1) * 128], identity=c.idb[:]),
                             reads=[W2.knb_b, c.idb_b], writes=[W2.tp_b], signal=(k == 7))
                    P.op("dve", lambda e: e.tensor_copy(out=W2.kT[:].rearrange("p k t -> p (k t)"), in_=W2.tp[:]),
                         reads=[W2.tp_b], writes=[W2.kT_b])
                    for a in range(2):
                        P.dma("sp", lambda e, a=a: e.dma_start(
                            out=kT_d[a::2, :, ch * 128:(ch + 1) * 128].rearrange("h d t -> d h t"),
                            in_=W2.kT[a * 64:(a + 1) * 64, :, :]), reads=[W2.kT_b], writes=[kv_b[gi]])
                    for k in range(8):
                        mm_group(P, W2.kmps[:, par * 8 + k:par * 8 + k + 1], W2.kmps_b,
                                 [(kn[:, k * 128:(k + 1) * 128], W2.ones32[:, 0:1])], [knb_, W2.ones32_b])
                    if par == 1:
                        for a in range(2):
                            dst = c.kmbd[a * 64:(a + 1) * 64, :, a * 32 + slot]
                            P.op("act", lambda e, a=a, dst=dst: e.activation(
                                out=dst, in_=W2.kmps[a * 64:(a + 1) * 64, 0:8], func=AF.Copy, scale=1.0 / BLK),
                                 reads=[W2.kmps_b], writes=[c.kmbd_b])
                            P.op("dve", lambda e, a=a, dst=dst: e.scalar_tensor_tensor(
                                out=dst, in0=W2.kmps[a * 64:(a + 1) * 64, 8:16], scalar=1.0 / BLK, in1=dst,
                                op0=ALU.mult, op1=ALU.add), reads=[W2.kmps_b, c.kmbd_b], writes=[c.kmbd_b])
                    if not own:
                        yield
                        return
                    qn, qnb = W2.qn32[par], W2.qn32_b[par]
                    jb = (ti * 4 + s) // 2
                    if (ti * 4 + s) % 2 == 0:
                        P.op("pool", lambda e: e.tensor_copy(out=W2.sbj[:], in_=W2.sbias[:]),
                             reads=[W2.sbias_b], writes=[W2.sbj_b])
                        P.op("pool", lambda e: e.memset(W2.sbj[:, 2 * jb + 1:32], -1e30), writes=[W2.sbj_b])
                    for kk in range(8):
                        P.op("pe", lambda e, kk=kk: e.transpose(
                            out=W2.tq[:, kk * 128:(kk + 1) * 128], in_=qn[:, kk * 128:(kk + 1) * 128],
                            identity=c.idf[:]), reads=[qnb, c.idf_b], writes=[W2.tq_b], signal=(kk == 7))
                    P.op("dve", lambda e: e.tensor_copy(
                        out=W2.qT32[:, 0:4, :].rearrange("p k t -> p (k t)"), in_=W2.tq[:, 0:512]),
                         reads=[W2.tq_b], writes=[W2.qT32_b])
                    P.op("act", lambda e: e.activation(
                        out=W2.qT32[:, 4:8, :].rearrange("p k t -> p (k t)"), in_=W2.tq[:, 512:1024], func=AF.Copy),
                         reads=[W2.tq_b], writes=[W2.qT32_b])
                    yield
                    for k in range(8):
                        P.op("pe", lambda e, k=k: e.matmul(out=W2.gps[:, k * 64:(k + 1) * 64], lhsT=W2.qT32[:, k, :],
                                                           rhs=c.kmbd[:, k, :], start=True, stop=True),
                             reads=[W2.qT32_b, c.kmbd_b], writes=[W2.gps_b], signal=(k == 7))
                    P.op("dve", lambda e: e.tensor_tensor(
                        out=W2.gate[:], in0=W2.gps[:].rearrange("p (h n) -> p h n", n=32),
                        in1=W2.sbj[:].unsqueeze(1).to_broadcast([128, NH, 32]), op=ALU.add),
                         reads=[W2.gps_b, W2.sbj_b], writes=[W2.gate_b])
                    for h in range(NH):
                        P.op("dve", lambda e, h=h: e.max(out=W2.top8[:, h, :], in_=W2.gate[:, h, :]),
                             reads=[W2.gate_b], writes=[W2.top8_b])
                    P.op("dve", lambda e: e.tensor_tensor(
                        out=W2.sel[:], in0=W2.gate[:], in1=W2.top8[:, :, TOPK - 1:TOPK].to_broadcast([128, NH, 32]),
                        op=ALU.is_ge), reads=[W2.gate_b, W2.top8_b], writes=[W2.sel_b])
                    P.op("dve", lambda e: e.tensor_single_scalar(out=W2.val[:], in_=W2.gate[:], scalar=-1e29,
                                                                 op=ALU.is_gt), reads=[W2.gate_b], writes=[W2.val_b])
                    P.op("dve", lambda e: e.tensor_tensor(out=W2.sel[:], in0=W2.sel[:], in1=W2.val[:], op=ALU.mult),
                         reads=[W2.sel_b, W2.val_b], writes=[W2.sel_b])
                    P.op("dve", lambda e: e.tensor_scalar(out=W2.qa[:, :, 64:96], in0=W2.sel[:], scalar1=-MASKNEG,
                                                         scalar2=MASKNEG, op0=ALU.mult, op1=ALU.add),
                         reads=[W2.sel_b], writes=[W2.qa_b])
                    P.op("dve", lambda e: e.memset(W2.qa[:, :, 64 + 2 * jb + 1:64 + 2 * jb + 2], 0.0),
                         writes=[W2.qa_b])
                    P.op("act", lambda e: e.activation(out=W2.qa[:, :, 0:64],
                                                       in_=qn[:].rearrange("p (h d) -> p h d", d=HD),
                                                       func=AF.Copy), reads=[qnb], writes=[W2.qa_b])
                    yield
                    for half in range(2):
                        for k in range(8):
                            h = half * 8 + k
                            P.op("pe", lambda e, k=k, h=h: e.transpose(out=W2.tp[0:96, k * 128:(k + 1) * 128],
                                                                       in_=W2.qa[:, h, :], identity=c.idb[:]),
                                 reads=[W2.qa_b, c.idb_b], writes=[W2.tp_b], signal=(k == 7))
                        P.op("dve", lambda e, half=half: e.tensor_copy(
                            out=W2.qaT[:, half * 8:(half + 1) * 8, :].rearrange("p k t -> p (k t)"), in_=W2.tp[0:96, :]),
                             reads=[W2.tp_b], writes=[W2.qaT_b])
                    tok0 = ti * 512 + s * 128
                    P.dma("sp", lambda e: e.dma_start(
                        out=qaT_d[:, :, tok0:tok0 + 128].rearrange("h r t -> r h t"), in_=W2.qaT[:]),
                          reads=[W2.qaT_b], writes=[qa_b[ti]])

                def stageF_list(gi, part):
                    own = gi >= NT
                    ti = gi - NT
                    hT, hTb = W2.hT[gi % 2], W2.hT_b[gi % 2]
                    fl = []

                    def conv_group(k):
                        pa, pab = feat_proj(hT, hTb, k * 128)
                        pbm, pbb = feat_proj(hT, hTb, 1024 + k * 128)
                        j = k % 2
                        P.op("act", lambda e: e.activation(out=W2.ev[j][:], in_=pbm[:], func=AF.Sigmoid),
                             reads=[pbb], writes=[W2.ev_b[j]])
                        P.op("dve", lambda e: e.tensor_tensor(out=W2.uo[j][:], in0=pa[:], in1=W2.ev[j][:], op=ALU.mult),
                             reads=[pab, W2.ev_b[j]], writes=[W2.uo_b[j]])
                        for bb in range(2):
                            jblk = 2 * (ti if own else gi) + bb
                            if own:
                                P.dma("sp", lambda e, bb=bb, jblk=jblk: e.dma_start(
                                    out=uT_d[k, :, jblk * 288 + 32:jblk * 288 + 288], in_=W2.uo[j][:, bb * 256:(bb + 1) * 256]),
                                      reads=[W2.uo_b[j]], writes=[u_b[ti]])
                            else:
                                P.dma("sp", lambda e, bb=bb, jblk=jblk: e.dma_start(
                                    out=uT_d[k, :, jblk * 288:jblk * 288 + 32], in_=W2.uo[j][:, bb * 256 + 224:bb * 256 + 256]),
                                      reads=[W2.uo_b[j]], writes=[u_b[NT + gi]])

                    def gate_group(gidx):
                        which, k = gidx // 8, gidx % 8
                        dst = gaT_d if which == 0 else gbT_d
                        pg, pgb = feat_proj(hT, hTb, 5120 + which * 1024 + k * 128)
                        j = k % 2
                        P.op("act", lambda e: e.activation(out=W2.go[j][:], in_=pg[:], func=AF.Sigmoid),
                             reads=[pgb], writes=[W2.go_b[j]])
                        P.dma("sp", lambda e: e.dma_start(out=dst[k, :, ti * 512:(ti + 1) * 512], in_=W2.go[j][:]),
                              reads=[W2.go_b[j]], writes=[g_b[ti]])

                    for k in (2 * part, 2 * part + 1):
                        fl.append(lambda k=k: conv_group(k))
                    if own:
                        for q4 in range(4):
                            fl.append(lambda g=part * 4 + q4: gate_group(g))
                    return fl

                seq = [(gi, s) for gi in range(2 * NT) for s in range(4)]
                load_hT(0)
                stageA(*seq[0])
                for idx, (gi, s) in enumerate(seq):
                    if s == 0 and gi + 1 < 2 * NT:
                        load_hT(gi + 1)
                    fl = stageF_list(gi, s)
                    nf = len(fl)
                    cuts = [0, (nf + 2) // 3, (2 * nf + 2) // 3, nf]
                    gen = stageB(gi, s)
                    next(gen, None)
                    for f in fl[cuts[0]:cuts[1]]:
                        f()
                    next(gen, None)
                    if idx + 1 < len(seq):
                        stageA(*seq[idx + 1])
                    for f in fl[cuts[1]:cuts[3]]:
                        f()
                    for _ in gen:
                        pass
                P.barrier()

        if "conv" in phases:
            with contextlib.ExitStack() as s3:
                W3 = Ctx()
                W3.wcp = sb(nc, s3, "p3wcp", [128, 8, D], BF16)
                W3.wcp_b = Buf()
                load_w_bf16(P, W3.wcp, W3.wcp_b, w_conv_proj, 8)
                W3.wc = sb(nc, s3, "p3wc", [128, 8, CW], F32)
                W3.wc_b = Buf()
                W3.wraw = sb(nc, s3, "p3wraw", [CW, D], F32)
                W3.wraw_b = Buf()
                P.dma("sp", lambda e: e.dma_start(out=W3.wraw[:], in_=conv_dw[:, :]), writes=[W3.wraw_b])
                W3.wtp = ps(nc, s3, "p3wtp", [128, 8, 32], F32)
                W3.wtp_b = Buf()
                for k in range(8):
                    P.op("pe", lambda e, k=k: e.transpose(out=W3.wtp[:, k, 0:CW], in_=W3.wraw[:, k * 128:(k + 1) * 128],
                                                          identity=c.idf[0:CW, 0:CW]),
                         reads=[W3.wraw_b, c.idf_b], writes=[W3.wtp_b], signal=(k == 7))
                P.op("dve", lambda e: e.tensor_copy(out=W3.wc[:], in_=W3.wtp[:, :, 0:CW]), reads=[W3.wtp_b], writes=[W3.wc_b])
                W3.bcol, W3.bcol_b = load_col(P, nc, s3, "p3bcol", conv_dw_bias, 8)
                W3.lg, W3.lg_b = load_col(P, nc, s3, "p3lg", conv_ln_gain, 8)
                W3.lb, W3.lb_b = load_col(P, nc, s3, "p3lb", conv_ln_bias, 8)
                W3.diag = sb(nc, s3, "p3diag", [128, 8 * CW, 128], BF16)
                W3.diag_b = Buf()
                for k in range(8):
                    eng = "pool" if k in (3, 7) else "dve"
                    P.op(eng, lambda e, k=k: e.tensor_tensor(
                        out=W3.diag[:, k * CW:(k + 1) * CW, :], in0=c.idf[:].unsqueeze(1).to_broadcast([128, CW, 128]),
                        in1=W3.wc[:, k, :].unsqueeze(2).to_broadcast([128, CW, 128]), op=ALU.mult),
                         reads=[c.idf_b, W3.wc_b], writes=[W3.diag_b])
                W3.onesm = sb(nc, s3, "p3onesm", [128, 128], BF16)
                W3.onesm_b = Buf()
                P.op("pool", lambda e: e.memset(W3.onesm[:], 1.0 / D), writes=[W3.onesm_b])
                W3.u = [sb(nc, s3, "p3u%d" % j, [128, 8, 576], BF16) for j in range(2)]
                W3.u_b = [Buf() for _ in range(2)]
                W3.y32 = [sb(nc, s3, "p3y32%d" % j, [128, 8, 512], F32) for j in range(2)]
                W3.y32_b = [[Buf() for _ in range(8)] for _ in range(2)]
                W3.ybf = [sb(nc, s3, "p3ybf%d" % j, [128, 8, 512], BF16) for j in range(2)]
                W3.ybf_b = [[Buf() for _ in range(8)] for _ in range(2)]
                W3.ysq = [sb(nc, s3, "p3ysq%d" % j, [128, 8, 512], BF16) for j in range(2)]
                W3.ysq_b = [[Buf() for _ in range(8)] for _ in range(2)]
                W3.mean = sb(nc, s3, "p3mean", [128, 512], F32)
                W3.mean_b = Buf()
                W3.m2 = sb(nc, s3, "p3m2", [128, 512], F32)
                W3.m2_b = Buf()
                W3.rstd = sb(nc, s3, "p3rstd", [128, 512], F32)
                W3.rstd_b = Buf()
                W3.d1 = [sb(nc, s3, "p3d1%d" % j, [128, 512], F32) for j in range(2)]
                W3.d1_b = [Buf() for _ in range(2)]
                W3.sT = sb(nc, s3, "p3sT", [128, 8, 512], BF16)
                W3.sT_b = [Buf() for _ in range(8)]
                W3.ga = sb(nc, s3, "p3ga", [128, 8, 512], F32)
                W3.ga_b = Buf()
                W3.ma = [sb(nc, s3, "p3ma%d" % j, [128, 512], F32) for j in range(2)]
                W3.ma_b = [Buf() for _ in range(2)]
                W3.cps = [ps(nc, s3, "p3cps%d" % j, [128, 512], F32) for j in range(2)]
                W3.cps_b = [Buf() for _ in range(2)]
                W3.mps = ps(nc, s3, "p3mps", [128, 512], F32)
                W3.mps_b = Buf()
                W3.qps = ps(nc, s3, "p3qps", [128, 512], F32)
                W3.qps_b = Buf()
                W3.yps = [ps(nc, s3, "p3yps%d" % j, [128, 512], F32) for j in range(2)]
                W3.yps_b = [Buf() for _ in range(2)]

                def conv_stage(ti):
                    tb = ti % 2
                    u, ub = W3.u[tb], W3.u_b[tb]
                    P.dma("sp", lambda e: e.dma_start(
                        out=u[:], in_=uT_d[:, :, ti * 576:(ti + 1) * 576].rearrange("k p t -> p k t")),
                          reads=[u_b[ti], u_b[NT + ti]], writes=[ub])
                    for k in range(8):
                        j = k % 2
                        for bb in range(2):
                            mm_group(P, W3.cps[j][:, bb * 256:(bb + 1) * 256], W3.cps_b[j],
                                     [(W3.diag[:, k * CW + jj, :], u[:, k, bb * 288 + 2 + jj:bb * 288 + 2 + jj + 256])
                                      for jj in range(CW)], [W3.diag_b, ub])
                        P.op("act", lambda e, k=k, j=j: e.activation(out=W3.y32[tb][:, k, :], in_=W3.cps[j][:], func=AF.Identity,
                                                                     bias=W3.bcol[:, k:k + 1]),
                             reads=[W3.cps_b[j], W3.bcol_b], writes=[W3.y32_b[tb][k]])
                        P.op("act", lambda e, k=k, j=j: e.activation(out=W3.ybf[tb][:, k, :], in_=W3.cps[j][:], func=AF.Identity,
                                                                     bias=W3.bcol[:, k:k + 1]),
                             reads=[W3.cps_b[j], W3.bcol_b], writes=[W3.ybf_b[tb][k]])
                        P.op("act", lambda e, k=k, j=j: e.activation(out=W3.ysq[tb][:, k, :], in_=W3.cps[j][:], func=AF.Square,
                                                                     bias=W3.bcol[:, k:k + 1]),
                             reads=[W3.cps_b[j], W3.bcol_b], writes=[W3.ysq_b[tb][k]])

                def ln_stage(ti):
                    tb = ti % 2
                    P.dma("sp", lambda e: e.dma_start(
                        out=W3.ga[:], in_=gaT_d[:, :, ti * 512:(ti + 1) * 512].rearrange("k p t -> p k t")),
                          reads=[g_b[ti]], writes=[W3.ga_b])
                    mm_group(P, W3.mps[:], W3.mps_b, [(W3.onesm[:], W3.ybf[tb][:, k, :]) for k in range(8)],
                             [W3.onesm_b] + W3.ybf_b[tb])
                    mm_group(P, W3.qps[:], W3.qps_b, [(W3.onesm[:], W3.ysq[tb][:, k, :]) for k in range(8)],
                             [W3.onesm_b] + W3.ysq_b[tb])
                    P.op("act", lambda e: e.activation(out=W3.mean[:], in_=W3.mps[:], func=AF.Copy),
                         reads=[W3.mps_b], writes=[W3.mean_b])
                    P.op("dve", lambda e: e.tensor_tensor(out=W3.m2[:], in0=W3.mean[:], in1=W3.mean[:], op=ALU.mult),
                         reads=[W3.mean_b], writes=[W3.m2_b])
                    P.op("dve", lambda e: e.scalar_tensor_tensor(out=W3.m2[:], in0=W3.qps[:], scalar=EPS, in1=W3.m2[:],
                                                                 op0=ALU.add, op1=ALU.subtract),
                         reads=[W3.qps_b, W3.m2_b], writes=[W3.m2_b])
                    P.op("act", lambda e: e.activation(out=W3.m2[:], in_=W3.m2[:], func=AF.Sqrt),
                         reads=[W3.m2_b], writes=[W3.m2_b])
                    P.op("dve", lambda e: e.reciprocal(out=W3.rstd[:], in_=W3.m2[:]),
                         reads=[W3.m2_b], writes=[W3.rstd_b])
                    for k in range(8):
                        j = k % 2
                        P.op("dve", lambda e, k=k, j=j: e.tensor_tensor(out=W3.d1[j][:], in0=W3.y32[tb][:, k, :], in1=W3.mean[:],
                                                                        op=ALU.subtract),
                             reads=[W3.y32_b[tb][k], W3.mean_b], writes=[W3.d1_b[j]])
                        P.op("pool", lambda e, j=j: e.tensor_tensor(out=W3.d1[j][:], in0=W3.d1[j][:], in1=W3.rstd[:],
                                                                    op=ALU.mult),
                             reads=[W3.d1_b[j], W3.rstd_b], writes=[W3.d1_b[j]])
                        P.op("act", lambda e, k=k, j=j: e.activation(out=W3.sT[:, k, :], in_=W3.d1[j][:], func=AF.Silu,
                                                                     scale=W3.lg[:, k:k + 1], bias=W3.lb[:, k:k + 1]),
                             reads=[W3.d1_b[j], W3.lg_b, W3.lb_b], writes=[W3.sT_b[k]])
                    for dk in range(8):
                        j = dk % 2
                        mm_group(P, W3.yps[j][:], W3.yps_b[j],
                                 [(W3.wcp[:, k, dk * 128:(dk + 1) * 128], W3.sT[:, k, :]) for k in range(8)],
                                 [W3.wcp_b] + W3.sT_b)
                        P.op("dve", lambda e, dk=dk, j=j: e.tensor_tensor(out=W3.ma[j][:], in0=W3.yps[j][:],
                                                                          in1=W3.ga[:, dk, :], op=ALU.mult),
                             reads=[W3.yps_b[j], W3.ga_b], writes=[W3.ma_b[j]])
                        P.dma("sp", lambda e, dk=dk, j=j: e.dma_start(
                            out=maT_d[dk, :, ti * 512:(ti + 1) * 512], in_=W3.ma[j][:]),
                              reads=[W3.ma_b[j]], writes=[ma_b[ti]])

                conv_stage(0)
                for ti in range(NT):
                    if ti + 1 < NT:
                        conv_stage(ti + 1)
                    ln_stage(ti)
                P.barrier()

        if "attn" in phases:
            with contextlib.ExitStack() as s4:
                W4 = Ctx()
                NKMAX = NTOK
                W4.wap = sb(nc, s4, "p4wap", [128, NH, D], BF16)
                W4.wap_b = Buf()
                P.op("pool", lambda e: e.memset(W4.wap[64:128, :, :], 0.0), writes=[W4.wap_b])
                P.dma("pool", lambda e: e.dma_start(out=W4.wap[0:64, :, :], in_=w_attn_proj.rearrange("(h d) n -> d h n", d=HD)),
                      writes=[W4.wap_b])
                W4.wo = sb(nc, s4, "p4wo", [128, 8, D], BF16)
                W4.wo_b = Buf()
                load_w_bf16(P, W4.wo, W4.wo_b, w_out, 8)
                W4.cb = sb(nc, s4, "p4cb", [128, 4, 512], BF16)
                W4.cb_b = Buf()
                W4.kh = [sb(nc, s4, "p4kh%d" % j, [96, NKMAX], BF16) for j in range(2)]
                W4.kh_b = [Buf() for _ in range(2)]
                P.op("pool", lambda e: e.memset(W4.cb[:], 0.0), writes=[W4.cb_b])
                for jb in range(2):
                    for kc in range(2):
                        P.op("pool", lambda e, jb=jb, kc=kc: e.affine_select(
                            out=W4.cb[:, jb * 2 + kc, jb * 256:(jb + 1) * 256],
                            in_=W4.cb[:, jb * 2 + kc, jb * 256:(jb + 1) * 256], pattern=[[1, 256]],
                            compare_op=ALU.is_ge, fill=MASKNEG, base=-kc * 128, channel_multiplier=-1),
                             reads=[W4.cb_b], writes=[W4.cb_b])
                for j in range(2):
                    khv = W4.kh[j][64:96, :].rearrange("p (s k) -> p s k", k=256)
                    P.op("pool", lambda e, khv=khv: e.iota(khv, [[1, 2 * NB], [0, 256]], base=0, channel_multiplier=-1,
                                                           allow_small_or_imprecise_dtypes=True), writes=[W4.kh_b[j]])
                    P.op("dve", lambda e, khv=khv: e.tensor_single_scalar(out=khv, in_=khv, scalar=0.0, op=ALU.is_equal),
                         reads=[W4.kh_b[j]], writes=[W4.kh_b[j]])
                W4.vh = [sb(nc, s4, "p4vh%d" % j, [128, NCH, HD + 1], BF16) for j in range(2)]
                W4.vh_b = [Buf() for _ in range(2)]
                W4.qh = [sb(nc, s4, "p4qh%d" % j, [96, 512], BF16) for j in range(2)]
                W4.qh_b = [Buf() for _ in range(2)]
                W4.pT = [sb(nc, s4, "p4pT%d" % j, [128, 512], BF16) for j in range(4)]
                W4.pT_b = [Buf() for _ in range(4)]
                W4.sps = [ps(nc, s4, "p4sps%d" % j, [128, 512], F32) for j in range(4)]
                W4.sps_b = [Buf() for _ in range(4)]
                W4.ops_ = [ps(nc, s4, "p4ops%d" % j, [128, 512], F32) for j in range(2)]
                W4.ops_b = [Buf() for _ in range(2)]
                W4.bps = ps(nc, s4, "p4bps", [128, 512], F32)
                W4.bps_b = Buf()
                _yps = ps(nc, s4, "p4yps0", [128, 512], F32)
                _ypsb = Buf()
                W4.yps = [_yps, _yps]
                W4.yps_b = [_ypsb, _ypsb]
                W4.rinv = sb(nc, s4, "p4rinv", [128, 512], F32)
                W4.rinv_b = Buf()
                W4.onesr = sb(nc, s4, "p4onesr", [128, 64], F32)
                W4.onesr_b = Buf()
                P.op("pool", lambda e: e.memset(W4.onesr[:], 1.0), writes=[W4.onesr_b])
                W4.bc = sb(nc, s4, "p4bc", [64, 512], F32)
                W4.bc_b = Buf()
                W4.oT = sb(nc, s4, "p4oT", [128, NH, 512], BF16)
                W4.oT_b = [Buf() for _ in range(NH)]
                P.op("pool", lambda e: e.memset(W4.oT[64:128, :, :], 0.0), writes=W4.oT_b)
                W4.ma = sb(nc, s4, "p4ma", [128, 8, 512], F32)
                W4.ma_b = Buf()
                W4.gb = sb(nc, s4, "p4gb", [128, 8, 512], F32)
                W4.gb_b = Buf()
                W4.x1 = sb(nc, s4, "p4x1", [128, 4, D], F32)
                W4.x1_b = [Buf() for _ in range(4)]
                W4.tmp = [sb(nc, s4, "p4tmp%d" % j, [128, 512], F32) for j in range(2)]
                W4.tmp_b = [Buf() for _ in range(2)]
                W4.mx = sb(nc, s4, "p4mx", [128, 8, 512], BF16)
                W4.mx_b = [Buf() for _ in range(8)]
                hi = 0
                si = 0
                pending = None
                for ti in range(NT):
                    nslots = 4 * ti + 4
                    nch = nslots * 2
                    nk = nch * 128
                    ownch = {8 * ti + 2: 0, 8 * ti + 3: 1, 8 * ti + 6: 2, 8 * ti + 7: 3}
                    for h in range(NH):
                        j = hi % 2
                        hi += 1
                        kh, khb, vh, vhb, qh, qhb = W4.kh[j], W4.kh_b[j], W4.vh[j], W4.vh_b[j], W4.qh[j], W4.qh_b[j]
                        P.dma("sp", lambda e, kh=kh, h=h, nk=nk: e.dma_start(out=kh[0:64, 0:nk], in_=kT_d[h, :, 0:nk]),
                              reads=kv_b, writes=[khb])
                        P.dma("sp", lambda e, vh=vh, h=h, nch=nch: e.dma_start(out=vh[:, 0:nch, :], in_=v_d[h, :, 0:nch, :]),
                              reads=kv_b, writes=[vhb])
                        P.dma("sp", lambda e, qh=qh, h=h, ti=ti: e.dma_start(out=qh[:], in_=qaT_d[h, :, ti * 512:(ti + 1) * 512]),
                              reads=[qa_b[ti]], writes=[qhb])
                        if h == 2:
                            P.dma("sp", lambda e, ti=ti: e.dma_start(
                                out=W4.ma[:], in_=maT_d[:, :, ti * 512:(ti + 1) * 512].rearrange("k p t -> p k t")),
                                  reads=[ma_b[ti]], writes=[W4.ma_b])
                            P.dma("sp", lambda e, ti=ti: e.dma_start(
                                out=W4.gb[:], in_=gbT_d[:, :, ti * 512:(ti + 1) * 512].rearrange("k p t -> p k t")),
                                  reads=[g_b[ti]], writes=[W4.gb_b])
                            for s in range(4):
                                P.dma("sp", lambda e, ti=ti, s=s: e.dma_start(
                                    out=W4.x1[:, s, :], in_=x1_d[ti * 512 + s * 128:ti * 512 + (s + 1) * 128, :]),
                                      reads=[x1_b[ti]], writes=[W4.x1_b[s]])
                        op_, opb = W4.ops_[h % 2], W4.ops_b[h % 2]
                        LA = 3
                        sjs = {}
                        for kc in range(nch + LA):
                            if kc < nch:
                                sj = si % 4
                                si += 1
                                sjs[kc] = sj
                                sp_, spb = W4.sps[sj], W4.sps_b[sj]
                                pairs = [(kh[:, kc * 128:(kc + 1) * 128], qh[:])]
                                rd = [khb, qhb]
                                if kc in ownch:
                                    pairs.append((c.idb[:], W4.cb[:, ownch[kc], :]))
                                    rd = rd + [c.idb_b, W4.cb_b]
                                mm_group(P, sp_[:], spb, pairs, rd)
                                P.op("act", lambda e, sj=sj, sp_=sp_: e.activation(out=W4.pT[sj][:], in_=sp_[:], func=AF.Exp,
                                                                                   scale=HD ** -0.5),
                                     reads=[spb], writes=[W4.pT_b[sj]])
                            if kc == min(12, nch - 1) and pending is not None:
                                pending()
                                pending = None
                            if kc >= LA:
                                k2 = kc - LA
                                sj = sjs[k2]
                                P.op("pe", lambda e, k2=k2, sj=sj, vh=vh, op_=op_, nch=nch: e.matmul(
                                    out=op_[0:65, :], lhsT=vh[:, k2, :], rhs=W4.pT[sj][:], start=(k2 == 0), stop=(k2 == nch - 1)),
                                     reads=[vhb, W4.pT_b[sj]], writes=[opb], signal=(k2 == nch - 1))

                        def norm_head(h=h, op_=op_, opb=opb):
                            P.op("dve", lambda e: e.reciprocal(out=W4.rinv[64:65, :], in_=op_[64:65, :]),
                                 reads=[opb], writes=[W4.rinv_b])
                            P.op("pe", lambda e: e.matmul(out=W4.bps[0:64, :], lhsT=W4.onesr[64:65, :], rhs=W4.rinv[64:65, :],
                                                          start=True, stop=True),
                                 reads=[W4.onesr_b, W4.rinv_b], writes=[W4.bps_b])
                            P.op("act", lambda e: e.activation(out=W4.bc[:], in_=W4.bps[0:64, :], func=AF.Copy),
                                 reads=[W4.bps_b], writes=[W4.bc_b])
                            P.op("dve", lambda e: e.tensor_tensor(out=W4.oT[0:64, h, :], in0=op_[0:64, :], in1=W4.bc[:],
                                                                  op=ALU.mult),
                                 reads=[opb, W4.bc_b], writes=[W4.oT_b[h]])
                        pending = norm_head
                    pending()
                    pending = None
                    for dk in range(8):
                        j = dk % 2
                        mm_group(P, W4.yps[j][:], W4.yps_b[j],
                                 [(W4.wap[:, h, dk * 128:(dk + 1) * 128], W4.oT[:, h, :]) for h in range(NH)],
                                 [W4.wap_b] + W4.oT_b)
                        P.op("dve", lambda e, dk=dk, j=j: e.tensor_tensor(out=W4.tmp[j][:], in0=W4.yps[j][:],
                                                                          in1=W4.gb[:, dk, :], op=ALU.mult),
                             reads=[W4.yps_b[j], W4.gb_b], writes=[W4.tmp_b[j]])
                        P.op("pool", lambda e, dk=dk, j=j: e.tensor_tensor(out=W4.mx[:, dk, :], in0=W4.tmp[j][:],
                                                                           in1=W4.ma[:, dk, :], op=ALU.add),
                             reads=[W4.tmp_b[j], W4.ma_b], writes=[W4.mx_b[dk]])
                    q = 0
                    for s in range(4):
                        for hf in range(2):
                            j = q % 2
                            q += 1
                            mm_group(P, W4.yps[j][:], W4.yps_b[j],
                                     [(W4.mx[:, k, s * 128:(s + 1) * 128], W4.wo[:, k, hf * 512:(hf + 1) * 512])
                                      for k in range(8)], [W4.wo_b] + W4.mx_b)
                            P.op("dve", lambda e, s=s, hf=hf, j=j: e.tensor_tensor(
                                out=W4.x1[:, s, hf * 512:(hf + 1) * 512], in0=W4.yps[j][:],
                                in1=W4.x1[:, s, hf * 512:(hf + 1) * 512], op=ALU.add),
                                 reads=[W4.yps_b[j], W4.x1_b[s]], writes=[W4.x1_b[s]])
                        P.dma("sp", lambda e, ti=ti, s=s: e.dma_start(
                            out=x2_d[ti * 512 + s * 128:ti * 512 + (s + 1) * 128, :], in_=W4.x1[:, s, :]),
                              reads=[W4.x1_b[s]], writes=[x2_b[ti]])
                P.barrier()

        if "ffn2" in phases:
            tiles = [dict(src=x2_d[t * 512:(t + 1) * 512, :], own=True, idx=t, buf=x2_b[t]) for t in range(NT)]

            def epi2(W, tile, s, x, xb):
                i = tile["idx"]
                P.dma("sp", lambda e: e.dma_start(out=out[i * 512 + s * 128:i * 512 + (s + 1) * 128, :], in_=x[:]),
                      reads=[xb], writes=[])

            ffn_phase(P, nc, c, "f2", tiles, ffn2_norm, ffn2_wg, ffn2_wu, ffn2_wd, epi2)

        P.finish()
        P.emit(block)
    return nc


WNAMES = ["ffn1_norm", "ffn1_w_gate", "ffn1_w_up", "ffn1_w_down", "mix_norm", "w_in", "conv_dw", "conv_dw_bias",
          "conv_ln_gain", "conv_ln_bias", "w_conv_proj", "q_norm", "k_norm", "w_attn_proj", "w_out", "ffn2_norm",
          "ffn2_w_gate", "ffn2_w_up", "ffn2_w_down"]


def make_in_maps(inputs, s_half):
    x = np.ascontiguousarray(inputs["x"], dtype=np.float32)
    B = x.shape[0]
    nb = s_half // BLK
    shared = {}
    for n in WNAMES:
        a = np.asarray(inputs[n], dtype=np.float32)
        a = a[0]
        if a.ndim == 1:
            a = a.reshape(1, -1)
        shared[n] = np.ascontiguousarray(a)
    maps = []
    for cid in range(2 * B):
        b, par = cid // 2, cid % 2
        xb = x[b].reshape(2 * nb, BLK, D)
        m = dict(shared)
        m["x_own"] = np.ascontiguousarray(xb[par::2].reshape(s_half, D))
        prev = np.zeros((nb, BLK, D), np.float32)
        for j in range(nb):
            g = 2 * j + par - 1
            if g >= 0:
                prev[j] = xb[g]
        m["x_prev"] = prev.reshape(s_half, D)
        sbias = np.zeros((1, 32), np.float32)
        if par == 0:
            sbias[0, 0] = -1e30
        m["slot_bias"] = sbias
        maps.append(m)
    return maps


def gather_out(results, x_shape, s_half):
    B = x_shape[0]
    nb = s_half // BLK
    outp = np.empty(x_shape, np.float32)
    for cid in range(2 * B):
        b, par = cid // 2, cid % 2
        ob = outp[b].reshape(2 * nb, BLK, D)
        ob[par::2] = np.asarray(results[cid]["out"]).reshape(nb, BLK, D)
    return outp


def kernel(**inputs):
    nc = bass.Bass("TRN2", target_bir_lowering=False)
    build(nc, S_HALF)
    maps = make_in_maps(inputs, S_HALF)
    res = run_bass_kernel_spmd(nc, maps, core_ids=list(range(NCORES)))
    return gather_out(res.results, inputs["x"].shape, S_HALF)
```
